# Optimizing a Trainium2 kernel written in Bass

```python
import math
import jax, jax.numpy as jnp
from jax import lax
import numpy as np

D_MODEL = 1024
BATCH = 32
SEQ = 2048
DEPTH = 4

GRID_W = 64
CTX_LEN = 256
HEAD_DIM = 64
Q_BLOCK = 128
EPS = 1e-6
ROPE_THETA = 10000.0
NEG_INF = -1e30

NA_HEADS = 4
NA_WIN_R = 8
NA_WIN_C = 16
NA_W = NA_HEADS * HEAD_DIM
DIFF_HEADS = 4
DIFF_DH = HEAD_DIM // 2
DIFF_W = DIFF_HEADS * 2 * DIFF_DH
DIFF_LAMBDA_STD = 0.1
GQA_HEADS = 4
GQA_KV_HEADS = 2
GQA_GROUP = GQA_HEADS // GQA_KV_HEADS
GQA_Q_W = GQA_HEADS * HEAD_DIM
GQA_KV_W = GQA_KV_HEADS * HEAD_DIM
MLA_HEADS = 4
MLA_Q_RANK = 192
MLA_KV_RANK = 128
MLA_NOPE = 64
MLA_ROPE = 32
MLA_V = 64
MLA_QK = MLA_NOPE + MLA_ROPE

N_BRANCH = 4
BRANCH_W = NA_W
IN_SIZES = (NA_W, NA_W, NA_W, DIFF_W, DIFF_W, DIFF_W, GQA_Q_W, GQA_KV_W, GQA_KV_W, MLA_Q_RANK, MLA_KV_RANK, MLA_ROPE)
IN_W = sum(IN_SIZES)
N_MOD = 6
D_FF = 2816
CONV_W = 3

kernel_name = "hybrid_parallel_dit_block"


def rms_norm(x, g):
    xf = x.astype(jnp.float32)
    y = xf * lax.rsqrt(jnp.mean(xf * xf, axis=-1, keepdims=True) + EPS)
    return (y * g.astype(jnp.float32)).astype(x.dtype)


def axial_rope(n_tokens, rot_dim):
    t = jnp.arange(n_tokens)
    row = (t // GRID_W).astype(jnp.float32)
    col = (t % GRID_W).astype(jnp.float32)
    n_axis = rot_dim // 4
    inv_freq = ROPE_THETA ** (-jnp.arange(n_axis, dtype=jnp.float32) / n_axis)
    ang = jnp.concatenate([row[:, None] * inv_freq, col[:, None] * inv_freq], axis=-1)
    return jnp.cos(ang), jnp.sin(ang)


def apply_rope(x, rope):
    cos, sin = rope
    half = x.shape[-1] // 2
    shp = (1, x.shape[1]) + (1,) * (x.ndim - 3) + (half,)
    c = cos.reshape(shp).astype(x.dtype)
    s = sin.reshape(shp).astype(x.dtype)
    x1, x2 = x[..., :half], x[..., half:]
    return jnp.concatenate([x1 * c - x2 * s, x1 * s + x2 * c], axis=-1)


def split_cols(p):
    outs, o = [], 0
    for n in IN_SIZES:
        outs.append(p[..., o:o + n])
        o += n
    return outs


def sweep_query_blocks(fn, q):
    b, s = q.shape[:2]
    nb = s // Q_BLOCK
    qb = jnp.moveaxis(q.reshape((b, nb, Q_BLOCK) + q.shape[2:]), 1, 0)
    out = lax.map(fn, qb)
    return jnp.moveaxis(out, 0, 1).reshape((b, s) + out.shape[3:])


def softmax_attention(q, k, v, scale):
    sc = jnp.einsum("bqhgd,bthd->bhgqt", q, k).astype(jnp.float32) * scale
    p = jax.nn.softmax(sc, axis=-1).astype(v.dtype)
    return jnp.einsum("bhgqt,bthd->bqhgd", p, v)


def neighborhood_attention(q, k, v, k_ctx, v_ctx, rel_bias, rows):
    b, s, h, d = q.shape
    win_r = min(NA_WIN_R, rows)
    scale = d ** -0.5
    qg = jnp.moveaxis(q.reshape(b, rows, GRID_W, h, d), 1, 0)
    kg = k.reshape(b, rows, GRID_W, h, d)
    vg = v.reshape(b, rows, GRID_W, h, d)
    q_rows = jnp.arange(rows)
    row_start = jnp.clip(q_rows - win_r // 2, 0, rows - win_r)
    cols = jnp.arange(GRID_W)
    col_start = jnp.clip(cols - NA_WIN_C // 2, 0, GRID_W - NA_WIN_C)
    col_in = (cols[None, :] >= col_start[:, None]) & (cols[None, :] < col_start[:, None] + NA_WIN_C)
    col_idx = jnp.clip(cols[None, :] - cols[:, None], -(NA_WIN_C - 1), NA_WIN_C - 1) + NA_WIN_C - 1
    n_band = win_r * GRID_W

    def one_row(args):
        r, rs, q_row = args
        k_band = lax.dynamic_slice_in_dim(kg, rs, win_r, axis=1)
        v_band = lax.dynamic_slice_in_dim(vg, rs, win_r, axis=1)
        row_idx = rs + jnp.arange(win_r) - r + NA_WIN_R - 1
        bias = rel_bias[:, row_idx[None, :, None], col_idx[:, None, :]]
        s_nb = jnp.einsum("bqhd,bikhd->bhqik", q_row, k_band).astype(jnp.float32) * scale + bias.astype(jnp.float32)
        s_nb = jnp.where(col_in[:, None, :], s_nb, NEG_INF)
        s_cx = jnp.einsum("bqhd,blhd->bhql", q_row, k_ctx).astype(jnp.float32) * scale
        p = jax.nn.softmax(jnp.concatenate([s_nb.reshape(b, h, GRID_W, n_band), s_cx], axis=-1), axis=-1).astype(v.dtype)
        p_nb = p[..., :n_band].reshape(b, h, GRID_W, win_r, GRID_W)
        return (jnp.einsum("bhqik,bikhd->bqhd", p_nb, v_band)
                + jnp.einsum("bhql,blhd->bqhd", p[..., n_band:], v_ctx))

    out = lax.map(one_row, (q_rows, row_start, qg))
    return jnp.moveaxis(out, 0, 1).reshape(b, s, h * d)


def branch_neighborhood(p, pc, rows, need_ctx, qk_g, rel_bias):
    q, k, v = [t.reshape(t.shape[:2] + (NA_HEADS, HEAD_DIM)) for t in p]
    qc, kc, vc = [t.reshape(t.shape[:2] + (NA_HEADS, HEAD_DIM)) for t in pc]
    q, k = rms_norm(q, qk_g[0]), rms_norm(k, qk_g[1])
    qc, kc = rms_norm(qc, qk_g[0]), rms_norm(kc, qk_g[1])
    y = neighborhood_attention(q, k, v, kc, vc, rel_bias, rows)
    yc = None
    if need_ctx:
        yc = softmax_attention(qc[:, :, :, None], kc, vc, HEAD_DIM ** -0.5).reshape(qc.shape[:2] + (NA_W,))
    return y, yc


def branch_differential(p, pc, rope, need_ctx, qk_g, lam_p, subln_g, lambda_init):
    def qkv(parts, rope_):
        q, k, v = parts
        q = rms_norm(q.reshape(q.shape[:2] + (DIFF_HEADS, 2, DIFF_DH)), qk_g[0])
        k = rms_norm(k.reshape(k.shape[:2] + (DIFF_HEADS, 2, DIFF_DH)), qk_g[1])
        v = v.reshape(v.shape[:2] + (DIFF_HEADS, 2 * DIFF_DH))
        if rope_ is not None:
            q, k = apply_rope(q, rope_), apply_rope(k, rope_)
        return q, k, v

    q, k, v = qkv(p, rope)
    qc, kc, vc = qkv(pc, None)
    lp = lam_p.astype(jnp.float32)
    lam = jnp.exp(jnp.sum(lp[0] * lp[1])) - jnp.exp(jnp.sum(lp[2] * lp[3])) + lambda_init
    scale = DIFF_DH ** -0.5

    def core(qb, k_, v_):
        sc = jnp.einsum("bqhmd,bthmd->bhmqt", qb, k_).astype(jnp.float32) * scale
        pr = jax.nn.softmax(sc, axis=-1)
        a = (pr[:, :, 0] - lam * pr[:, :, 1]).astype(v_.dtype)
        return jnp.einsum("bhqt,bthd->bqhd", a, v_)

    def post(o):
        return (rms_norm(o, subln_g) * (1.0 - lambda_init)).reshape(o.shape[:2] + (DIFF_W,))

    k_all = jnp.concatenate([k, kc], axis=1)
    v_all = jnp.concatenate([v, vc], axis=1)
    y = post(sweep_query_blocks(lambda qb: core(qb, k_all, v_all), q))
    yc = post(core(qc, kc, vc)) if need_ctx else None
    return y, yc


def branch_gqa(p, pc, rope, need_ctx, qk_g):
    def qkv(parts, rope_):
        q, k, v = parts
        q = rms_norm(q.reshape(q.shape[:2] + (GQA_HEADS, HEAD_DIM)), qk_g[0])
        k = rms_norm(k.reshape(k.shape[:2] + (GQA_KV_HEADS, HEAD_DIM)), qk_g[1])
        v = v.reshape(v.shape[:2] + (GQA_KV_HEADS, HEAD_DIM))
        if rope_ is not None:
            q, k = apply_rope(q, rope_), apply_rope(k, rope_)
        return q.reshape(q.shape[:2] + (GQA_KV_HEADS, GQA_GROUP, HEAD_DIM)), k, v

    q, k, v = qkv(p, rope)
    qc, kc, vc = qkv(pc, None)
    scale = HEAD_DIM ** -0.5
    k_all = jnp.concatenate([k, kc], axis=1)
    v_all = jnp.concatenate([v, vc], axis=1)
    y = sweep_query_blocks(lambda qb: softmax_attention(qb, k_all, v_all, scale), q)
    y = y.reshape(y.shape[:2] + (GQA_Q_W,))
    yc = None
    if need_ctx:
        yc = softmax_attention(qc, kc, vc, scale)
        yc = yc.reshape(yc.shape[:2] + (GQA_Q_W,))
    return y, yc


def branch_mla(p, pc, rope, need_ctx, qa_g, kva_g, w_qb, w_kvb, qk_g):
    def qkv(parts, rope_):
        qa, kva, kr = parts
        bb, t = qa.shape[:2]
        q = (rms_norm(qa, qa_g) @ w_qb).reshape(bb, t, MLA_HEADS, MLA_QK)
        kv = (rms_norm(kva, kva_g) @ w_kvb).reshape(bb, t, MLA_HEADS, MLA_NOPE + MLA_V)
        q_nope = rms_norm(q[..., :MLA_NOPE], qk_g[0, :MLA_NOPE])
        q_rope = rms_norm(q[..., MLA_NOPE:], qk_g[0, MLA_NOPE:])
        k_nope = rms_norm(kv[..., :MLA_NOPE], qk_g[1, :MLA_NOPE])
        k_rope = rms_norm(kr, qk_g[1, MLA_NOPE:])
        if rope_ is not None:
            q_rope, k_rope = apply_rope(q_rope, rope_), apply_rope(k_rope, rope_)
        q = jnp.concatenate([q_nope, q_rope], axis=-1)[:, :, :, None]
        k = jnp.concatenate([k_nope, jnp.broadcast_to(k_rope[:, :, None], (bb, t, MLA_HEADS, MLA_ROPE))], axis=-1)
        return q, k, kv[..., MLA_NOPE:]

    q, k, v = qkv(p, rope)
    qc, kc, vc = qkv(pc, None)
    scale = MLA_QK ** -0.5
    k_all = jnp.concatenate([k, kc], axis=1)
    v_all = jnp.concatenate([v, vc], axis=1)
    y = sweep_query_blocks(lambda qb: softmax_attention(qb, k_all, v_all, scale), q)
    y = y.reshape(y.shape[:2] + (MLA_HEADS * MLA_V,))
    yc = None
    if need_ctx:
        yc = softmax_attention(qc, kc, vc, scale)
        yc = yc.reshape(yc.shape[:2] + (MLA_HEADS * MLA_V,))
    return y, yc


def merge_branches(h, branches, w_gate, w_branch, w_out):
    y = None
    for i in range(N_BRANCH):
        term = jax.nn.sigmoid(h @ w_gate[i]) * (branches[i] @ w_branch[i])
        y = term if y is None else y + term
    return y @ w_out


def depthwise_conv_centered(u, w, bias):
    t = u.shape[1]
    pad = CONV_W // 2
    up = jnp.pad(u, ((0, 0), (pad, pad), (0, 0)))
    out = bias
    for j in range(CONV_W):
        out = out + up[:, j:j + t] * w[j]
    return out


def conv_ffn(h, w_up, conv_w, conv_b, w_down):
    u = depthwise_conv_centered(h @ w_up, conv_w, conv_b)
    gate, val = u[..., :D_FF], u[..., D_FF:]
    return (jax.nn.silu(gate) * val) @ w_down


def setup_inputs(seed: int = 0) -> dict:
    key = jax.random.key(seed)
    ks = iter(jax.random.split(key, 40))

    def nrm(shape, scale):
        return jax.random.normal(next(ks), shape, jnp.float32) * scale

    def gain(shape):
        return 1.0 + nrm(shape, 0.02)

    L, D = DEPTH, D_MODEL
    return {
        "x": nrm((BATCH, SEQ, D), 1.0),
        "c": nrm((BATCH, D), 1.0),
        "ctx": nrm((BATCH, CTX_LEN, D), 1.0),
        "c_ctx": nrm((D,), 1.0),
        "w_mod": nrm((L, D, N_MOD * D), 0.3 * D ** -0.5),
        "b_mod": nrm((L, N_MOD * D), 0.02),
        "norm1_g": gain((L, D)),
        "norm2_g": gain((L, D)),
        "w_in": nrm((L, D, IN_W), D ** -0.5),
        "na_qk_g": gain((L, 2, HEAD_DIM)),
        "na_rel_bias": nrm((L, NA_HEADS, 2 * NA_WIN_R - 1, 2 * NA_WIN_C - 1), 0.02),
        "diff_qk_g": gain((L, 2, DIFF_DH)),
        "diff_lambda": nrm((L, 4, DIFF_DH), DIFF_LAMBDA_STD),
        "diff_subln_g": gain((L, 2 * DIFF_DH)),
        "gqa_qk_g": gain((L, 2, HEAD_DIM)),
        "mla_qa_g": gain((L, MLA_Q_RANK)),
        "mla_kva_g": gain((L, MLA_KV_RANK)),
        "w_mla_qb": nrm((L, MLA_Q_RANK, MLA_HEADS * MLA_QK), MLA_Q_RANK ** -0.5),
        "w_mla_kvb": nrm((L, MLA_KV_RANK, MLA_HEADS * (MLA_NOPE + MLA_V)), MLA_KV_RANK ** -0.5),
        "mla_qk_g": gain((L, 2, MLA_QK)),
        "w_gate": nrm((L, N_BRANCH, D, D), D ** -0.5),
        "w_branch": nrm((L, N_BRANCH, BRANCH_W, D), BRANCH_W ** -0.5),
        "w_out": nrm((L, D, D), D ** -0.5),
        "w_up": nrm((L, D, 2 * D_FF), D ** -0.5),
        "conv_w": nrm((L, CONV_W, 2 * D_FF), CONV_W ** -0.5),
        "conv_b": nrm((L, 2 * D_FF), 0.02),
        "w_down": nrm((L, D_FF, D), D_FF ** -0.5),
    }


def reference(x, c, ctx, c_ctx, w_mod, b_mod, norm1_g, norm2_g, w_in, na_qk_g, na_rel_bias,
              diff_qk_g, diff_lambda, diff_subln_g, gqa_qk_g, mla_qa_g, mla_kva_g, w_mla_qb, w_mla_kvb,
              mla_qk_g, w_gate, w_branch, w_out, w_up, conv_w, conv_b, w_down):
    b, s, d = x.shape
    rows = s // GRID_W
    rope_diff = axial_rope(s, DIFF_DH)
    rope_gqa = axial_rope(s, HEAD_DIM)
    rope_mla = axial_rope(s, MLA_ROPE)
    c_act = jax.nn.silu(c)
    cc_act = jax.nn.silu(c_ctx)
    xc = ctx
    for i in range(DEPTH):
        need_ctx = i < DEPTH - 1
        lambda_init = 0.8 - 0.6 * math.exp(-0.3 * i)
        mod = (c_act @ w_mod[i] + b_mod[i]).reshape(b, N_MOD, 1, d)
        modc = (cc_act @ w_mod[i] + b_mod[i]).reshape(N_MOD, 1, d)
        h = rms_norm(x, norm1_g[i]) * (1 + mod[:, 1]) + mod[:, 0]
        hc = rms_norm(xc, norm1_g[i]) * (1 + modc[1]) + modc[0]
        p = split_cols(h @ w_in[i])
        pc = split_cols(hc @ w_in[i])
        ya, yac = branch_neighborhood(p[0:3], pc[0:3], rows, need_ctx, na_qk_g[i], na_rel_bias[i])
        yb, ybc = branch_differential(p[3:6], pc[3:6], rope_diff, need_ctx, diff_qk_g[i], diff_lambda[i],
                                      diff_subln_g[i], lambda_init)
        yg, ygc = branch_gqa(p[6:9], pc[6:9], rope_gqa, need_ctx, gqa_qk_g[i])
        ym, ymc = branch_mla(p[9:12], pc[9:12], rope_mla, need_ctx, mla_qa_g[i], mla_kva_g[i],
                             w_mla_qb[i], w_mla_kvb[i], mla_qk_g[i])
        x = x + mod[:, 2] * merge_branches(h, (ya, yb, yg, ym), w_gate[i], w_branch[i], w_out[i])
        h2 = rms_norm(x, norm2_g[i]) * (1 + mod[:, 4]) + mod[:, 3]
        x = x + mod[:, 5] * conv_ffn(h2, w_up[i], conv_w[i], conv_b[i], w_down[i])
        if need_ctx:
            xc = xc + modc[2] * merge_branches(hc, (yac, ybc, ygc, ymc), w_gate[i], w_branch[i], w_out[i])
            hc2 = rms_norm(xc, norm2_g[i]) * (1 + modc[4]) + modc[3]
            xc = xc + modc[5] * conv_ffn(hc2, w_up[i], conv_w[i], conv_b[i], w_down[i])
    return x
```

```python
import math
from contextlib import ExitStack
import numpy as np
import concourse.bass as bass
import concourse.mybir as mybir
from concourse.bass_utils import run_bass_kernel_spmd

F32 = mybir.dt.float32
BF16 = mybir.dt.bfloat16
AF = mybir.ActivationFunctionType
ALU = mybir.AluOpType
AX = mybir.AxisListType

D = 1024
SEQ = 2048
CTX = 256
T = SEQ + CTX
DEPTH = 4
NCORES = 8
GRID_W = 64
EPS = 1e-6
THETA = 10000.0
D_FF = 2816
NJ = D_FF // 128
IN_W = 2400
O_NA_Q, O_NA_K, O_NA_V = 0, 256, 512
O_DF_Q, O_DF_K, O_DF_V = 768, 1024, 1280
O_GQ_Q, O_GQ_K, O_GQ_V = 1536, 1792, 1920
O_ML_QA, O_ML_KVA, O_ML_KR = 2048, 2240, 2368
TT = [(0, 512), (512, 512), (1024, 512), (1536, 512), (2048, 256)]
FF_GROUPS = [list(range(0, 5)), list(range(5, 10)), list(range(10, 14)), list(range(14, 18)), list(range(18, 22))]
NEG = -30000.0

C_G1, C_G2 = 0, 8
C_NA_GQ, C_NA_GK, C_DF_GQ, C_DF_GK, C_GQ_GQ, C_GQ_GK = 16, 17, 18, 19, 20, 21
C_ML_GQA0, C_ML_GQA1, C_ML_GKVA, C_ML_GQ, C_ML_GK, C_DF_SUB = 22, 23, 24, 25, 26, 27
C_CONV = 28
NCOL = C_CONV + 2 * NJ * 4
K_MLO, K_MHI, K_DM0, K_INVD_ML, K_EPS = 0, 1, 2, 6, 7
NKC = 8
DC_NA_LO, DC_NA_HI, DC_DF_GQ, DC_GQ_GQ, DC_ML_GQ, DC_DF_SUB, DC_NEGLAM = 0, 1, 2, 3, 4, 5, 6
NDC = 8
M_ONES, M_O64, M_O32, M_OML, M_PDF, M_PGQ, M_PML = 0, 1, 2, 3, 0, 1, 2
NMAT = 7
NA_KT = {0: list(range(0, 6)), 1: list(range(2, 10)), 2: list(range(6, 14)), 3: list(range(10, 16))}


def _weight_layout():
    names = []

    def add(n, c):
        names.append((n, c))

    for br in ("na", "df"):
        for n in ("q0", "q1", "k0", "k1", "va", "vb"):
            add(f"{br}_{n}", 1024)
    for n in ("qA", "qB", "k", "v"):
        add(f"gq_{n}", 1024)
    add("ml_qa0", 1024)
    add("ml_qa1", 512)
    add("ml_kva", 1024)
    add("ml_kr", 256)
    add("ml_2nd", 1280)
    for oc in range(8):
        for i in range(4):
            add(f"gb_{i}_{oc}", 1280)
    for oc in range(8):
        add(f"wo_{oc}", 1024)
    for j in range(NJ):
        add(f"up_g_{j}", 1024)
        add(f"up_v_{j}", 1024)
    for g, js in enumerate(FF_GROUPS):
        for oc in range(8):
            add(f"dn_{g}_{oc}", len(js) * 128)
    off = {}
    o = 0
    for n, c in names:
        off[n] = (o, c)
        o += c
    return names, off, o


W_NAMES, W_OFF, WCOLS = _weight_layout()
CAST_PIECE = 8192
N_PIECES = (WCOLS + CAST_PIECE - 1) // CAST_PIECE


def _tilew(W):
    K, M = W.shape
    nk = (K + 127) // 128
    if nk * 128 != K:
        Wp = np.zeros((nk * 128, M), np.float32)
        Wp[:K] = W
    else:
        Wp = W
    return Wp.reshape(nk, 128, M).transpose(1, 0, 2).reshape(128, nk * M)


def _build_blob(l, w_in, w_mla_qb, w_mla_kvb, w_gate, w_branch, w_out, w_up, w_down):
    blob = np.empty((128, WCOLS), np.float32)

    def put(name, arr):
        o, c = W_OFF[name]
        assert arr.shape == (128, c), (name, arr.shape, c)
        blob[:, o:o + c] = arr

    wi = w_in[l]
    for br, oq, ok, ov in (("na", O_NA_Q, O_NA_K, O_NA_V), ("df", O_DF_Q, O_DF_K, O_DF_V)):
        put(f"{br}_q0", _tilew(wi[:, oq:oq + 128]))
        put(f"{br}_q1", _tilew(wi[:, oq + 128:oq + 256]))
        put(f"{br}_k0", _tilew(wi[:, ok:ok + 128]))
        put(f"{br}_k1", _tilew(wi[:, ok + 128:ok + 256]))
        put(f"{br}_va", _tilew(wi[0:512, ov:ov + 256]))
        put(f"{br}_vb", _tilew(wi[512:1024, ov:ov + 256]))
    qa = np.concatenate([wi[:, O_GQ_Q:O_GQ_Q + 64], wi[:, O_GQ_Q + 128:O_GQ_Q + 192]], axis=1)
    qb = np.concatenate([wi[:, O_GQ_Q + 64:O_GQ_Q + 128], wi[:, O_GQ_Q + 192:O_GQ_Q + 256]], axis=1)
    put("gq_qA", _tilew(qa))
    put("gq_qB", _tilew(qb))
    put("gq_k", _tilew(wi[:, O_GQ_K:O_GQ_K + 128]))
    put("gq_v", _tilew(wi[:, O_GQ_V:O_GQ_V + 128]))
    put("ml_qa0", _tilew(wi[:, O_ML_QA:O_ML_QA + 128]))
    put("ml_qa1", _tilew(wi[:, O_ML_QA + 128:O_ML_QA + 192]))
    put("ml_kva", _tilew(wi[:, O_ML_KVA:O_ML_KVA + 128]))
    put("ml_kr", _tilew(wi[:, O_ML_KR:O_ML_KR + 32]))
    parts = []
    for h in range(4):
        parts.append(_tilew(w_mla_qb[l][:, h * 96:(h + 1) * 96]))
    for h in range(4):
        parts.append(_tilew(w_mla_kvb[l][:, h * 128:h * 128 + 64]))
    parts.append(_tilew(np.concatenate([w_mla_kvb[l][:, h * 128 + 64:h * 128 + 128] for h in range(4)], axis=1)))
    put("ml_2nd", np.concatenate(parts, axis=1))
    gq_perm = np.concatenate([np.arange(0, 64), np.arange(128, 192), np.arange(64, 128), np.arange(192, 256)])
    for i in range(4):
        wb = w_branch[l, i]
        if i == 2:
            wb = wb[gq_perm]
        for oc in range(8):
            g = _tilew(w_gate[l, i][:, oc * 128:(oc + 1) * 128])
            b = _tilew(wb[:, oc * 128:(oc + 1) * 128])
            put(f"gb_{i}_{oc}", np.concatenate([g, b], axis=1))
    for oc in range(8):
        put(f"wo_{oc}", _tilew(w_out[l][:, oc * 128:(oc + 1) * 128]))
    for j in range(NJ):
        put(f"up_g_{j}", _tilew(w_up[l][:, j * 128:(j + 1) * 128]))
        put(f"up_v_{j}", _tilew(w_up[l][:, D_FF + j * 128:D_FF + (j + 1) * 128]))
    for g, js in enumerate(FF_GROUPS):
        rows = w_down[l][js[0] * 128:(js[-1] + 1) * 128]
        for oc in range(8):
            put(f"dn_{g}_{oc}", _tilew(rows[:, oc * 128:(oc + 1) * 128]))
    return blob


def _build_cols(l, norm1_g, norm2_g, na_qk_g, diff_qk_g, diff_subln_g, gqa_qk_g, mla_qa_g, mla_kva_g,
                mla_qk_g, conv_w, conv_b):
    c = np.zeros((128, NCOL), np.float32)
    c[:, C_G1:C_G1 + 8] = norm1_g[l].reshape(8, 128).T
    c[:, C_G2:C_G2 + 8] = norm2_g[l].reshape(8, 128).T
    c[:, C_NA_GQ] = np.tile(na_qk_g[l, 0], 2)
    c[:, C_NA_GK] = np.tile(na_qk_g[l, 1], 2)
    c[:, C_DF_GQ] = np.tile(diff_qk_g[l, 0], 4)
    c[:, C_DF_GK] = np.tile(diff_qk_g[l, 1], 4)
    c[:, C_GQ_GQ] = np.tile(gqa_qk_g[l, 0], 2)
    c[:, C_GQ_GK] = np.tile(gqa_qk_g[l, 1], 2)
    c[:, C_ML_GQA0] = mla_qa_g[l, 0:128]
    c[0:64, C_ML_GQA1] = mla_qa_g[l, 128:192]
    c[:, C_ML_GKVA] = mla_kva_g[l]
    c[0:96, C_ML_GQ] = mla_qk_g[l, 0]
    c[0:96, C_ML_GK] = mla_qk_g[l, 1]
    c[:, C_DF_SUB] = np.tile(diff_subln_g[l], 2)
    for kind in range(2):
        for j in range(NJ):
            lo = kind * D_FF + j * 128
            base = C_CONV + (kind * NJ + j) * 4
            for tap in range(3):
                c[:, base + tap] = conv_w[l, tap, lo:lo + 128]
            c[:, base + 3] = conv_b[l, lo:lo + 128]
    return c


def _const_cols():
    k = np.zeros((128, NKC), np.float32)
    p = np.arange(128)
    k[:, K_MLO] = (p < 64)
    k[:, K_MHI] = (p >= 64)
    for g in range(4):
        k[:, K_DM0 + g] = (p // 32 == g)
    k[0:64, K_INVD_ML] = 1.0 / 64
    k[64:128, K_INVD_ML] = 1.0 / 32
    k[:, K_EPS] = EPS
    return k


def _const_mats():
    m = np.zeros((NMAT, 128, 128), np.float32)
    p = np.arange(128)
    m[M_ONES] = 1.0
    m[M_O64] = (p[:, None] // 64 == p[None, :] // 64)
    m[M_O32] = (p[:, None] // 32 == p[None, :] // 32)
    blk = np.where(p < 64, 0, np.where(p < 96, 1, 2 + p))
    m[M_OML] = (blk[:, None] == blk[None, :])

    def perm(G, lo, hi):
        a = np.zeros((128, 128), np.float32)
        h = G // 2
        for mm_ in range(lo, hi):
            loc = (mm_ - lo) % G
            if loc < h:
                a[mm_ + h, mm_] = -1.0
            else:
                a[mm_ - h, mm_] = 1.0
        return a

    m[4 + M_PDF] = perm(32, 0, 128)
    m[4 + M_PGQ] = perm(64, 0, 128)
    m[4 + M_PML] = perm(32, 64, 96)
    return np.ascontiguousarray(m.transpose(1, 0, 2))


def _rope_tables():
    t = np.arange(SEQ)
    row = (t // GRID_W).astype(np.float32)
    col = (t % GRID_W).astype(np.float32)

    def ang(rot_dim):
        n_axis = rot_dim // 4
        inv = (np.float32(THETA) ** (-np.arange(n_axis, dtype=np.float32) / np.float32(n_axis))).astype(np.float32)
        return np.concatenate([row[:, None] * inv, col[:, None] * inv], axis=-1).astype(np.float32)

    tab = np.zeros((3, 2, 128, SEQ), np.float32)
    p = np.arange(128)
    a = ang(32)
    tab[0, 0] = np.cos(a)[:, p % 16].T
    tab[0, 1] = np.sin(a)[:, p % 16].T
    a = ang(64)
    tab[1, 0] = np.cos(a)[:, p % 32].T
    tab[1, 1] = np.sin(a)[:, p % 32].T
    a = ang(32)
    tab[2, 0] = 1.0
    tab[2, 0, 64:96] = np.cos(a)[:, (p[64:96] - 64) % 16].T
    tab[2, 1, 64:96] = np.sin(a)[:, (p[64:96] - 64) % 16].T
    return tab


NA_PATS = [(j, kt) for j in range(4) for kt in NA_KT[j]]
NA_PAT_IDX = {jk: i for i, jk in enumerate(NA_PATS)}
NA_NF = 22


def _na_bias_tables(na_rel_bias):
    L = na_rel_bias.shape[0]
    p = np.arange(128)
    kl = p // 64
    kc = p % 64
    fp = np.arange(NA_NF)
    qc = np.arange(64)
    dr = kl[:, None] + 10 - fp[None, :]
    cs = np.clip(qc - 8, 0, 48)
    colv = (kc[:, None] >= cs[None, :]) & (kc[:, None] < cs[None, :] + 16)
    valid = ((dr >= -7) & (dr <= 7))[:, :, None] & colv[:, None, :]
    ri = np.clip(dr + 7, 0, 14)[:, :, None] + np.zeros((1, 1, 64), np.int64)
    ci = (np.clip(kc[:, None] - qc[None, :], -15, 15) + 15)[:, None, :] + np.zeros((1, NA_NF, 1), np.int64)
    g = na_rel_bias[:, :, ri, ci]
    return np.ascontiguousarray(np.where(valid[None, None], g, np.float32(NEG)).astype(np.float32))


def _na_row_masks():
    rm = np.zeros((128, len(NA_PATS), 8), np.float32)
    p = np.arange(128)
    for i, (j, kt) in enumerate(NA_PATS):
        kr = 2 * kt + p // 64
        qr = 8 * j + np.arange(8)
        rs = np.clip(qr - 4, 0, 24)
        ok = (kr[:, None] >= rs[None, :]) & (kr[:, None] < rs[None, :] + 8)
        rm[:, i, :] = np.where(ok, 0.0, NEG)
    return rm


ENGS = ("pe", "act", "dve", "pool", "sp")
N_DMA_SEMS = 40
SEM_WRAP = 30000


class Sched:
    def __init__(self, nc):
        self.nc = nc
        self.ops = []
        self.last_w = {}
        self.readers = {}
        self.stack = ExitStack()

    def sb(self, name, shape, dt):
        return self.stack.enter_context(self.nc.sbuf_tensor(name, list(shape), dt))

    def ps(self, name, shape, dt=F32):
        return self.stack.enter_context(self.nc.psum_tensor(name, list(shape), dt))

    def op(self, eng, fn, reads=(), writes=(), dma=False):
        i = len(self.ops)
        deps = set()
        for k in reads:
            w = self.last_w.get(k)
            if w is not None:
                deps.add(w)
            if type(k) is tuple and k[0] == "ps":
                for r in self.readers.get(k, ()):
                    if self.ops[r][0] != eng:
                        deps.add(r)
        for k in writes:
            w = self.last_w.get(k)
            if w is not None:
                deps.add(w)
            r = self.readers.get(k)
            if r:
                deps.update(r)
        for k in reads:
            self.readers.setdefault(k, []).append(i)
        for k in writes:
            self.last_w[k] = i
            self.readers[k] = []
        ops = self.ops
        deps = [d for d in deps if dma or ops[d][3] or ops[d][0] != eng or eng != "pe"]
        ops.append([eng, fn, deps, dma, False, None])
        return i

    def emit(self, final_wait_ops=()):
        nc = self.nc
        ops = self.ops
        for o in ops:
            for d in o[2]:
                ops[d][4] = True
        for i in final_wait_ops:
            ops[i][4] = True
        cnt = {e: 0 for e in ENGS}
        nsem = {e: 1 for e in ENGS}
        dma_k = [0] * N_DMA_SEMS
        nd = 0
        for o in ops:
            if o[3]:
                s = nd % N_DMA_SEMS
                nd += 1
                dma_k[s] += 1
                o[5] = ("d", s, 16 * dma_k[s])
            elif o[4]:
                e = o[0]
                cnt[e] += 1
                ep, c = divmod(cnt[e] - 1, SEM_WRAP)
                o[5] = (e, ep, c + 1)
                nsem[e] = max(nsem[e], ep + 1)
        sems = {}
        for e in ENGS:
            for ep in range(nsem[e]):
                sems[(e, ep)] = self.stack.enter_context(nc.semaphore(f"s_{e}_{ep}"))
        for s in range(min(N_DMA_SEMS, max(nd, 1))):
            sems[("d", s)] = self.stack.enter_context(nc.semaphore(f"s_dma_{s}"))
        block = self.stack.enter_context(nc.Block())
        by_eng = {e: [] for e in ENGS}
        for i, o in enumerate(ops):
            by_eng[o[0]].append(i)

        def make(e):
            def body(eng):
                waited = {}
                for i in by_eng[e]:
                    o = ops[i]
                    need = {}
                    for d in o[2]:
                        t = ops[d][5]
                        key = (t[0], t[1])
                        if need.get(key, 0) < t[2]:
                            need[key] = t[2]
                    if o[3]:
                        t = o[5]
                        if t[2] > 16:
                            key = (t[0], t[1])
                            if need.get(key, 0) < t[2] - 16:
                                need[key] = t[2] - 16
                    for key, v in need.items():
                        if waited.get(key, 0) < v:
                            eng.wait_ge(sems[key], v)
                            waited[key] = v
                    ins = o[1](eng)
                    if o[3]:
                        t = o[5]
                        ins.then_inc(sems[(t[0], t[1])], 16)
                    elif o[4]:
                        t = o[5]
                        ins.then_inc(sems[(t[0], t[1])], 1)
                if e == "sp":
                    for i in final_wait_ops:
                        t = ops[i][5]
                        eng.wait_ge(sems[(t[0], t[1])], t[2])
            return body

        block.tensor(make("pe"))
        block.scalar(make("act"))
        block.vector(make("dve"))
        block.gpsimd(make("pool"))
        block.sync(make("sp"))

    def close(self):
        self.stack.close()


class Ring:
    def __init__(self, items, key, keys=None):
        self.items = items
        self.keys = keys if keys is not None else [(key, k) for k in range(len(items))]
        self.i = 0

    def get(self):
        k = self.i % len(self.items)
        self.i += 1
        return self.items[k], self.keys[k]


def build_program(NB, layers, lambda_inits, debug=False, skip_ffn=False):
    L = len(layers)
    nc = bass.Bass("TRN2", target_bir_lowering=False)
    S = Sched(nc)

    def dram_in(name, shape, dt=F32):
        return nc.dram_tensor(name, list(shape), dt, kind="ExternalInput").ap()

    xT_d = dram_in("xT", [NB, 128, 8, T])
    cT_d = dram_in("cT", [128, 8, 5])
    wmod_d = dram_in("wmod", [L, 48, 128, 1024])
    bmod_d = dram_in("bmod", [128, L, 48])
    wblob_d = dram_in("wblob", [L, 128, WCOLS])
    cols_d = dram_in("cols", [128, L, NCOL])
    kcol_d = dram_in("kcol", [128, NKC])
    cmat_d = dram_in("cmat", [128, NMAT, 128])
    rope_d = dram_in("rope", [3, 2, 128, SEQ])
    nab_d = dram_in("nab", [L, 4, 128, NA_NF * 64])
    narm_d = dram_in("narm", [128, len(NA_PATS) * 8])
    lam_d = dram_in("lam", [128, L, 128])
    yT_d = nc.dram_tensor("yT", [NB, 128, 8, T], F32, kind="ExternalOutput").ap()
    wbf_d = nc.dram_tensor("wbf", [L, 128, WCOLS], BF16, kind="Internal").ap()
    ybr_d = nc.dram_tensor("ybr", [5, 128, 8, 512], BF16, kind=("ExternalOutput" if debug else "Internal")).ap()

    X = S.sb("X", [128, 8, T], F32)
    H = S.sb("H", [128, 8, T], BF16)
    NSLOT = 9
    AR = S.sb("AR", [128, NSLOT, T], BF16)
    NWR = 4
    WRC = 1280
    WR = S.sb("WR", [128, NWR, WRC], BF16)
    NTF = 8
    TFt = S.sb("TF", [128, NTF, 512], F32)
    NTB = 5
    TBt = S.sb("TB", [128, NTB, 512], BF16)
    RTt = S.sb("RT", [128, 1024 + NA_NF * 64], F32)
    RMt = S.sb("RMT", [128, len(NA_PATS) * 8], F32)
    COLS = S.sb("COLS", [128, L, NCOL], F32)
    KCOL = S.sb("KCOL", [128, NKC], F32)
    DCOL = S.sb("DCOL", [128, L, NDC], F32)
    CMF = S.sb("CMF", [128, 3, 128], F32)
    CMB = S.sb("CMB", [128, 4, 128], BF16)
    MODR = S.sb("MODR", [128, L, 48, 5], F32)
    AMOD = S.sb("AMOD", [128, L, 2, 8, 5], F32)
    BMOD = S.sb("BMOD", [128, L, 48], F32)
    CT = S.sb("CT", [128, 8, 5], F32)
    LTMP = S.sb("LTMP", [128, 8], F32)
    YST = S.sb("YST", [128, 5, 512], BF16)

    PS = [S.ps(f"ps{i}", [128, 512]) for i in range(8)]
    def psring(idx):
        return Ring([PS[i] for i in idx], "ps", [("ps", i) for i in idx])

    PSm_small = psring([0, 1])
    import os as _os
    PSm_big = psring([0, 1, 3, 4, 5, 6, 7]) if _os.environ.get("PSM_BIG", "1") == "1" else PSm_small

    class _PSm:
        cur = PSm_big

        @staticmethod
        def get():
            return _PSm.cur.get()

    PSm = _PSm
    PSx_small = psring([2])
    PSx_big = psring([2, 4, 5, 6, 7])
    PSm_proj = psring([0, 1, 3])

    class _PSx:
        cur = PSx_small

        @staticmethod
        def get():
            return _PSx.cur.get()

    PSx = _PSx
    PSS = psring([3, 4, 5])
    PSo = psring([6, 7])

    def mode_proj():
        PSm.cur = PSm_proj
        PSx.cur = PSx_big

    def mode_attn():
        Pipe.drain()
        PSm.cur = PSm_small
        PSx.cur = PSx_small

    def mode_dense():
        Pipe.drain()
        PSm.cur = PSm_big
        PSx.cur = PSx_small
    TF = Ring([TFt[:, i, :] for i in range(NTF)], "tf")
    TB = Ring([TBt[:, i, :] for i in range(NTB)], "tb")
    RT = Ring([RTt[:, i * 1024:(i + 1) * 1024].rearrange("p (a b) -> p a b", a=2) for i in range(2)], "rt")
    NAT = RTt[:, 1024:1024 + NA_NF * 64]
    NATK = ("rt", 1)
    RSD = RTt[:, 512:1024]
    WRr = Ring([WR[:, i, :] for i in range(NWR)], "wr")

    op = S.op
    dmaq = ["sp"]

    def mm(out, lhsT, rhs, start, stop, reads, writes):
        return op("pe", lambda e: e.matmul(out, lhsT=lhsT, rhs=rhs, start=start, stop=stop), reads, writes)

    def act(out, in_, func, reads, writes, scale=1.0, bias=0.0):
        return op("act", lambda e: e.activation(out=out, in_=in_, func=func, bias=bias, scale=scale), reads, writes)

    def tt(eng, out, in0, in1, alu, reads, writes):
        return op(eng, lambda e: e.tensor_tensor(out=out, in0=in0, in1=in1, op=alu), reads, writes)

    def ts(eng, out, in0, s1, s2, op0, op1, reads, writes):
        if s2 is None:
            return op(eng, lambda e: e.tensor_scalar(out=out, in0=in0, scalar1=s1, scalar2=0.0, op0=op0, op1=ALU.add),
                      reads, writes)
        return op(eng, lambda e: e.tensor_scalar(out=out, in0=in0, scalar1=s1, scalar2=s2, op0=op0, op1=op1), reads, writes)

    def stt(out, in0, scalar, in1, op0, op1, reads, writes):
        return op("dve", lambda e: e.scalar_tensor_tensor(out=out, in0=in0, scalar=scalar, in1=in1, op0=op0, op1=op1),
                  reads, writes)

    def dma(out, in_, reads, writes, eng="sp"):
        return op(eng, lambda e: e.dma_start(out=out, in_=in_), reads, writes, dma=True)

    def kcol(i, rows=slice(0, 128)):
        return KCOL[rows, i:i + 1]

    def lcol(li, i, rows=slice(0, 128)):
        return COLS[rows, li, i:i + 1]

    def dcol(li, i, rows=slice(0, 128)):
        return DCOL[rows, li, i:i + 1]

    dma(COLS[:], cols_d, [], ["cols"])
    dma(KCOL[:], kcol_d, [], ["kcol"])
    dma(CMF[:], cmat_d[:, 4:7, :], [], ["cmf"])
    dma(CT[:], cT_d, [], ["ct"])
    dma(BMOD[:], bmod_d, [], ["bmod"])
    dma(RMt[:], narm_d, [], ["rmt"])
    op("pool", lambda e: e.memset(TFt[:], 0.0), [], [("tf", i) for i in range(NTF)])
    op("pool", lambda e: e.memset(TBt[:], 0.0), [], [("tb", i) for i in range(NTB)])
    op("pool", lambda e: e.memset(RTt[:], 0.0), [], [("rt", 0), ("rt", 1)])
    ones_st, onk = TF.get()
    dma(ones_st[:, 0:512], cmat_d[:, 0:4, :].rearrange("p a b -> p (a b)"), [onk], [onk])
    op("dve", lambda e: e.tensor_copy(out=CMB[:].rearrange("p a b -> p (a b)"), in_=ones_st[:, 0:512]), [onk], ["cmb"])
    LAMt, lamk = TF.get()
    dma(LAMt[:, 0:L * 128], lam_d.rearrange("p l k -> p (l k)"), [lamk], [lamk])
    LAM = LAMt[:, 0:L * 128].rearrange("p (l k) -> p l k", k=128)
    act(CT[:], CT[:], AF.Silu, ["ct"], ["ct"])
    for li in range(L):
        lam_init = lambda_inits[li]
        ts("dve", dcol(li, DC_NA_LO), lcol(li, C_NA_GQ), 0.125, kcol(K_MLO), ALU.mult, ALU.mult, ["cols", "kcol"], ["dcol"])
        ts("dve", dcol(li, DC_NA_HI), lcol(li, C_NA_GQ), 0.125, kcol(K_MHI), ALU.mult, ALU.mult, ["cols", "kcol"], ["dcol"])
        ts("dve", dcol(li, DC_DF_GQ), lcol(li, C_DF_GQ), 32.0 ** -0.5, None, ALU.mult, None, ["cols"], ["dcol"])
        ts("dve", dcol(li, DC_GQ_GQ), lcol(li, C_GQ_GQ), 0.125, None, ALU.mult, None, ["cols"], ["dcol"])
        ts("dve", dcol(li, DC_ML_GQ), lcol(li, C_ML_GQ), 96.0 ** -0.5, None, ALU.mult, None, ["cols"], ["dcol"])
        ts("dve", dcol(li, DC_DF_SUB), lcol(li, C_DF_SUB), 1.0 - lam_init, None, ALU.mult, None, ["cols"], ["dcol"])
        tt("dve", LAM[:, li, 0:32], LAM[:, li, 0:32], LAM[:, li, 32:64], ALU.mult, [lamk], [lamk])
        tt("dve", LAM[:, li, 64:96], LAM[:, li, 64:96], LAM[:, li, 96:128], ALU.mult, [lamk], [lamk])
        op("dve", lambda e, li=li: e.tensor_reduce(out=LTMP[:, 0:1], in_=LAM[:, li, 0:32], axis=AX.X, op=ALU.add),
           [lamk], ["ltmp"])
        op("dve", lambda e, li=li: e.tensor_reduce(out=LTMP[:, 1:2], in_=LAM[:, li, 64:96], axis=AX.X, op=ALU.add),
           [lamk], ["ltmp"])
        act(LTMP[:, 2:4], LTMP[:, 0:2], AF.Exp, ["ltmp"], ["ltmp"])
        tt("dve", LTMP[:, 4:5], LTMP[:, 3:4], LTMP[:, 2:3], ALU.subtract, ["ltmp"], ["ltmp"])
        ts("dve", dcol(li, DC_NEGLAM), LTMP[:, 4:5], -lam_init, None, ALU.add, None, ["ltmp"], ["dcol"])
    def emit_cast(li):
        for pc in range(N_PIECES):
            c0 = pc * CAST_PIECE
            c1 = min(WCOLS, c0 + CAST_PIECE)
            dma(wbf_d[li, :, c0:c1], wblob_d[li, :, c0:c1], [], [("wbf", li, pc)], eng="pool")

    emit_cast(0)
    def emit_mod(li):
        pm, pmk = PSx.get()
        for oc in range(48):
            wt, wk = TF.get()
            wt2, wk2 = TF.get()
            dma(wt[:, :], wmod_d[li, oc, :, 0:512], [], [wk])
            dma(wt2[:, :], wmod_d[li, oc, :, 512:1024], [], [wk2])
            for kc in range(8):
                src = wt if kc < 4 else wt2
                kk = kc % 4
                mm(pm[:, oc * 5:oc * 5 + 5], src[:, kk * 128:(kk + 1) * 128], CT[:, kc, :], kc == 0, kc == 7,
                   [wk, wk2, "ct"], [pmk])
        for j in range(5):
            tt("dve", MODR[:, li, :, j], pm[:, 0:240].rearrange("p (o j) -> p o j", j=5)[:, :, j], BMOD[:, li, :],
               ALU.add, [pmk, "bmod"], [("modr", li)])
        for which, (mi, gi) in enumerate(((1, C_G1), (4, C_G2))):
            for c in range(8):
                ts("dve", AMOD[:, li, which, c, :], MODR[:, li, mi * 8 + c, :], 1.0, lcol(li, gi + c),
                   ALU.add, ALU.mult, [("modr", li), "cols"], [("amod", li)])

    emit_mod(0)

    def wload(li, name):
        o, c = W_OFF[name]
        slot, key = WRr.get()
        p0 = o // CAST_PIECE
        p1 = (o + c - 1) // CAST_PIECE
        q = dmaq[0]
        dma(slot[:, 0:c], wbf_d[li, :, o:o + c], [("wbf", li, p) for p in range(p0, p1 + 1)], [key], eng=q)
        return slot, key

    def slot(i):
        return AR[:, i, :]

    def skey(i, ti):
        return ("ar", i, ti)

    def xkeys(c, ti):
        return ("x", c, ti)

    def hkeys(ti):
        return [("h", c, ti) for c in range(8)]

    def modcol(li, m, c, j):
        return MODR[:, li, m * 8 + c, j:j + 1]

    def norm_phase(li, b, which, tiles=(0, 1, 2, 3, 4)):
        for ti in tiles:
            t0, n = TT[ti]
            j = b if ti < 4 else 4
            st, stk = PSx.get()
            for c in range(8):
                sq, sqk = TB.get()
                act(sq[:, :n], X[:, c, t0:t0 + n], AF.Square, [xkeys(c, ti)], [sqk])
                mm(st[:, :n], CMB[:, M_ONES, :], sq[:, :n], c == 0, c == 7, [sqk, "cmb"], [stk])
            ln, lnk = TF.get()
            act(ln[:, :n], st[:, :n], AF.Ln, [stk, "kcol"], [lnk], scale=1.0 / D, bias=kcol(K_EPS))
            rs, rsk = RSD, ("rt", 0)
            act(rs[:, :n], ln[:, :n], AF.Exp, [lnk], [rsk], scale=-0.5)
            for c in range(8):
                tmp, tk = TF.get()
                stt(tmp[:, :n], X[:, c, t0:t0 + n], AMOD[:, li, which, c, j:j + 1], rs[:, :n], ALU.mult, ALU.mult,
                    [xkeys(c, ti), ("amod", li), rsk], [tk])
                act(H[:, c, t0:t0 + n], tmp[:, :n], AF.Identity, [tk, ("modr", li)], [("h", c, ti)],
                    bias=modcol(li, 3 * which, c, j))

    class Pipe:
        items = []

        @staticmethod
        def add(stages):
            it = Pipe.items
            it.append(stages)
            k = len(it) - 1
            stages[0]()
            if k >= 2:
                it[k - 2][2]()
            if k >= 1:
                it[k - 1][1]()

        @staticmethod
        def drain():
            it = Pipe.items
            k = len(it)
            if k >= 2:
                it[k - 2][2]()
            if k >= 1:
                it[k - 1][1]()
                it[k - 1][2]()
            Pipe.items = []

    def mask_write(di, out, in_, col, reads, writes):
        e = ("act", "pool", "dve", "act")[di % 4]
        if e == "act":
            act(out, in_, AF.Identity, reads, writes, scale=col)
        else:
            ts(e, out, in_, col, None, ALU.mult, None, reads, writes)

    def project_norm(parts, M, row0, stats_m, invd, gain0, dests, rope, tis, reads_extra, ti_reads):
        rows = slice(row0, row0 + M)
        for ti in tis:
            t0, n = TT[ti]
            st_ = {}

            def s1(ti=ti, t0=t0, n=n, st_=st_):
                ps, psk = PSm.get()
                for idx, (lh, rf) in enumerate(parts):
                    mm(ps[rows, :n], lh, rf(t0, n), idx == 0, idx == len(parts) - 1, reads_extra + ti_reads(ti), [psk])
                sq, sqk = TB.get()
                act(sq[rows, :n], ps[rows, :n], AF.Square, [psk], [sqk])
                st_.update(raw=ps, rawk=psk, sq=sq, sqk=sqk)

            def s2(ti=ti, t0=t0, n=n, st_=st_):
                raw, rawk, sq, sqk = st_["raw"], st_["rawk"], st_["sq"], st_["sqk"]
                st, stk = PSx.get()
                mm(st[:, :n], CMB[:, stats_m, :], sq[:, :n], True, True, [sqk, "cmb"], [stk])
                rs, rsk = TF.get()
                act(rs[rows, :n], st[rows, :n], AF.Ln, [stk, "kcol"], [rsk], scale=invd, bias=kcol(K_EPS, rows))
                act(rs[rows, :n], rs[rows, :n], AF.Exp, [rsk], [rsk], scale=-0.5)
                if rope is None or ti == 4:
                    xh = None
                    for di, (dfn, kfn, col) in enumerate(dests):
                        g = col if rope is None else gain0
                        if rope is not None and col is not None:
                            if xh is None:
                                xh, xhk = TF.get()
                                stt(xh[rows, :n], raw[rows, :n], gain0, rs[rows, :n], ALU.mult, ALU.mult,
                                    [rawk, rsk, "cols", "dcol"], [xhk])
                            mask_write(di, dfn(t0, n), xh[rows, :n], col, [xhk, "kcol"], [kfn(ti)])
                        else:
                            stt(dfn(t0, n), raw[rows, :n], g, rs[rows, :n], ALU.mult, ALU.mult,
                                [rawk, rsk, "cols", "dcol"], [kfn(ti)])
                else:
                    pmat, tab = rope
                    xh, xhk = TF.get()
                    stt(xh[rows, :n], raw[rows, :n], gain0, rs[rows, :n], ALU.mult, ALU.mult,
                        [rawk, rsk, "cols", "dcol"], [xhk])
                    rt, rtk = RT.get()
                    dma(rt[:, 0, :n], rope_d[tab, 0, :, t0:t0 + n], [], [rtk])
                    dma(rt[:, 1, :n], rope_d[tab, 1, :, t0:t0 + n], [], [rtk])
                    st_.update(rs=rs, rsk=rsk, xh=xh, xhk=xhk, rt=rt, rtk=rtk)

            def s3(ti=ti, t0=t0, n=n, st_=st_):
                if rope is None or ti == 4:
                    return
                pmat, tab = rope
                xh, xhk, rt, rtk = st_["xh"], st_["xhk"], st_["rt"], st_["rtk"]
                t1, t1k = TF.get()
                t2, t2k = st_["rs"], st_["rsk"]
                pr, prk = PSx.get()
                mm(pr[:, :n], CMF[:, pmat, :], xh[:, :n], True, True, [xhk, "cmf"], [prk])
                tt("pool", t1[rows, :n], xh[rows, :n], rt[rows, 0, :n], ALU.mult, [xhk, rtk], [t1k])
                tt("dve", t2[rows, :n], pr[rows, :n], rt[rows, 1, :n], ALU.mult, [prk, rtk], [t2k])
                if len(dests) == 1 and dests[0][2] is None:
                    dfn, kfn, _ = dests[0]
                    tt("dve", dfn(t0, n), t1[rows, :n], t2[rows, :n], ALU.add, [t1k, t2k], [kfn(ti)])
                else:
                    tt("dve", t1[rows, :n], t1[rows, :n], t2[rows, :n], ALU.add, [t1k, t2k], [t1k])
                    for di, (dfn, kfn, col) in enumerate(dests):
                        mask_write(di, dfn(t0, n), t1[rows, :n], col, [t1k, "kcol"], [kfn(ti)])

            Pipe.add((s1, s2, s3))

    def hparts(wt, M, c0=0):
        return [(wt[:, c0 + kc * M:c0 + (kc + 1) * M], (lambda kc: lambda t0, n: H[:, kc, t0:t0 + n])(kc)) for kc in range(8)]

    def sdest(si, rows=slice(0, 128)):
        return (lambda t0, n: AR[rows, si, t0:t0 + n]), (lambda ti: skey(si, ti))

    ALL_TI = [0, 1, 2, 3, 4]

    def v_project(parts_fn, ncols, pairs):
        Pipe.drain()
        VA = AR[:, 6:9, :].rearrange("p s t -> p (s t)").rearrange("p (k c) -> p k c", c=384)
        for tk in range(18):
            ti = min(tk // 4, 4)
            ps, psk = PSm.get()
            parts, rd = parts_fn(tk * 128, ti)
            for idx, (lh, rh) in enumerate(parts):
                mm(ps[:, :ncols], lh, rh, idx == 0, idx == len(parts) - 1, rd, [psk])
            src = ps[:, :ncols].rearrange("p (a b d) -> p a b d", b=2, d=64)
            dst = VA[:, tk, 0:pairs * 192].rearrange("p (a r) -> p a r", r=192)
            for b2 in range(2):
                if tk % 2 == 0:
                    op("act", (lambda b2, src, dst: lambda e: e.copy(out=dst[:, :, b2 * 128:b2 * 128 + 64], in_=src[:, :, b2, :]))(b2, src, dst),
                       [psk], [("va", tk)])
                else:
                    op("dve", (lambda b2, src, dst: lambda e: e.tensor_copy(out=dst[:, :, b2 * 128:b2 * 128 + 64], in_=src[:, :, b2, :]))(b2, src, dst),
                       [psk], [("va", tk)])
        return VA

    def attention(li, q_fn, k_fn, VA, pair, odd, qtiles, out_fn, q_reads, k_reads, head=None, hook=None):
        col0 = pair * 192 + (64 if odd else 0)
        num = slice(64, 128) if odd else slice(0, 64)
        den = slice(0, 64) if odd else slice(64, 128)
        for (ti, kts, pat0) in qtiles:
            t0, n = TT[ti]
            o, ok = PSo.get()
            sb = {}

            def emitS(idx):
                ps, psk = PSS.get()
                kt = kts[idx]
                mm(ps[:, :n], k_fn(kt), q_fn(t0, n), True, True, q_reads(ti) + k_reads(kt), [psk])
                sb[idx] = (ps, psk)

            emitS(0)
            if len(kts) > 1:
                emitS(1)
            for idx, kt in enumerate(kts):
                if idx + 2 < len(kts):
                    emitS(idx + 2)
                ps, psk = sb.pop(idx)
                p, pk = TB.get()
                if pat0 is not None and kt < 16:
                    pi = NA_PAT_IDX[(ti, kt)]
                    f0 = 10 - 2 * kt + 8 * ti
                    tmp, tmpk = TF.get()
                    tmp3 = tmp[:, :n].rearrange("p (r c) -> p r c", c=64)
                    nat3 = NAT[:, f0 * 64:f0 * 64 + 512].rearrange("p (r c) -> p r c", c=64)
                    rmb = RMt[:, pi * 8:pi * 8 + 8].unsqueeze(2).to_broadcast([128, 8, 64])
                    op("pool", (lambda tmp3, nat3, rmb: lambda e: e.tensor_tensor(out=tmp3, in0=nat3, in1=rmb, op=ALU.add))(tmp3, nat3, rmb),
                       [NATK, "rmt"], [tmpk])
                    tt("dve", tmp[:, :n], ps[:, :n], tmp[:, :n], ALU.add, [psk, tmpk], [tmpk])
                    act(p[:, :n], tmp[:, :n], AF.Exp, [tmpk], [pk])
                else:
                    act(p[:, :n], ps[:, :n], AF.Exp, [psk], [pk])
                mm(o[:, :n], VA[:, kt, col0:col0 + 128], p[:, :n], idx == 0, idx == len(kts) - 1,
                   [pk, ("va", kt)], [ok])
                if hook is not None and idx == min(12, len(kts) - 1):
                    hook()
                    hook = None
            rd, rdk = TF.get()
            if head is not None:
                act(rd[den, :n], o[den, :n], AF.Ln, [ok], [rdk])
                act(rd[den, :n], rd[den, :n], AF.Exp, [rdk], [rdk], scale=-1.0)
            else:
                op("dve", (lambda rd, o, n: lambda e: e.reciprocal(out=rd[den, :n], in_=o[den, :n]))(rd, o, n), [ok], [rdk])
            out_fn(ti, t0, n, o, ok, rd, rdk, num, den)


    def branch_out_writer(i_br, c2):
        state = {}

        def out_fn(ti, t0, n, o, ok, rd, rdk, num, den):
            state[ti] = True
            yt, ytk = YST[:, ti, :], ("yst", ti)
            tt("dve", yt[num, :n], o[num, :n], rd[den, :n], ALU.mult, [ok, rdk], [ytk])

        def flush(ti):
            t0, n = TT[ti]
            state.pop(ti)
            yt, ytk = YST[:, ti, :], ("yst", ti)
            dma(ybr_d[ti, :, i_br * 2 + c2, 0:n], yt[:, :n], [ytk], [("ybr", ti)])
        return out_fn, flush

    full_kts = list(range(18))
    ctx_kts = [16, 17]

    def std_qtiles():
        return [(ti, full_kts, None) for ti in range(4)] + [(4, ctx_kts, None)]

    def layer(li, b):
        mode_dense()
        if b == 0 and li + 1 < L:
            emit_cast(li + 1)
            emit_mod(li + 1)
        norm_phase(li, b, 0)
        mode_proj()
        hr = lambda ti: hkeys(ti)
        VAv = AR[:, 6:9, :].rearrange("p s t -> p (s t)").rearrange("p (k c) -> p k c", c=384)
        for pr_ in range(2):
            op("pool", (lambda pr_: lambda e: e.memset(VAv[:, :, pr_ * 192 + 64:pr_ * 192 + 128], 1.0))(pr_), [],
               [("va", i) for i in range(18)] + [skey(s_, tq) for s_ in (6, 7, 8) for tq in range(5)])
        for c in range(2):
            wq, wqk = wload(li, f"na_q{c}")
            project_norm(hparts(wq, 128), 128, 0, M_O64, 1.0 / 64, None,
                         [sdest(c * 3 + 0) + (dcol(li, DC_NA_LO),), sdest(c * 3 + 1) + (dcol(li, DC_NA_HI),)],
                         None, ALL_TI, [wqk], hr)
            wk_, wkk = wload(li, f"na_k{c}")
            project_norm(hparts(wk_, 128), 128, 0, M_O64, 1.0 / 64, None,
                         [sdest(c * 3 + 2) + (lcol(li, C_NA_GK),)], None, ALL_TI, [wkk], hr)
        wva, wvak = wload(li, "na_va")
        wvb, wvbk = wload(li, "na_vb")

        def na_vparts(tk0, ti):
            ps = []
            for kc in range(8):
                w = wva if kc < 4 else wvb
                ps.append((H[:, kc, tk0:tk0 + 128], w[:, (kc % 4) * 256:(kc % 4 + 1) * 256]))
            return ps, [wvak, wvbk] + hkeys(ti)

        VA = v_project(na_vparts, 256, 2)
        mode_attn()
        for c in range(2):
            ofn, flush = branch_out_writer(0, c)
            qt = [(ti, NA_KT[ti] + ctx_kts, 1) for ti in range(4)] + [(4, ctx_kts, None)]
            for hh in range(2):
                si = c * 3 + hh
                dma(NAT, nab_d[li, 2 * c + hh, :, :], [], [NATK])
                attention(li, (lambda si: lambda t0, n: AR[:, si, t0:t0 + n])(si),
                          (lambda c: lambda kt: AR[:, c * 3 + 2, kt * 128:(kt + 1) * 128])(c),
                          VA, c, hh == 1, qt, ofn,
                          (lambda si: lambda ti_: [skey(si, ti_)])(si),
                          (lambda c: lambda kt: [skey(c * 3 + 2, min(kt // 4, 4))])(c), head=2 * c + hh)
            for ti in range(5):
                flush(ti)
        mode_proj()
        wva, wvak = wload(li, "df_va")
        wvb, wvbk = wload(li, "df_vb")

        def df_vparts(tk0, ti):
            ps = []
            for kc in range(8):
                w = wva if kc < 4 else wvb
                ps.append((H[:, kc, tk0:tk0 + 128], w[:, (kc % 4) * 256:(kc % 4 + 1) * 256]))
            return ps, [wvak, wvbk] + hkeys(ti)

        VA = v_project(df_vparts, 256, 2)
        for c in range(2):
            mode_proj()
            wq, wqk = wload(li, f"df_q{c}")
            project_norm(hparts(wq, 128), 128, 0, M_O32, 1.0 / 32, dcol(li, DC_DF_GQ),
                         [sdest(g) + (kcol(K_DM0 + g),) for g in range(4)], (M_PDF, 0), ALL_TI, [wqk], hr)
            wk_, wkk = wload(li, f"df_k{c}")
            project_norm(hparts(wk_, 128), 128, 0, M_O32, 1.0 / 32, lcol(li, C_DF_GK),
                         [sdest(4) + (None,)], (M_PDF, 0), ALL_TI, [wkk], hr)
            mode_attn()
            state = {}

            def df_out(ti, t0, n, o, ok, rd, rdk, num, den, state=state):
                ym, ymk = TF.get()
                tt("dve", ym[num, :n], o[num, :n], rd[den, :n], ALU.mult, [ok, rdk], [ymk])
                state.setdefault(ti, []).append((ym, ymk))

            pending = []

            def df_post(ti, hh, y0, y0k, y1, y1k, c=c):
                t0, n = TT[ti]
                num = slice(64, 128) if hh else slice(0, 64)
                yt, ytk = YST[:, ti, :], ("yst", ti)
                yd, ydk = TF.get()
                stt(yd[num, :n], y1[num, :n], dcol(li, DC_NEGLAM, num), y0[num, :n], ALU.mult, ALU.add,
                    [y0k, y1k, "dcol"], [ydk])
                sq, sqk = TB.get()
                tt("pool", sq[:, :n], yd[:, :n], yd[:, :n], ALU.mult, [ydk], [sqk])
                st, stk = PSx.get()
                mm(st[:, :n], CMB[:, M_O64, :], sq[:, :n], True, True, [sqk, "cmb"], [stk])
                ln, lnk = TF.get()
                act(ln[num, :n], st[num, :n], AF.Ln, [stk, "kcol"], [lnk], scale=1.0 / 64, bias=kcol(K_EPS, num))
                act(ln[num, :n], ln[num, :n], AF.Exp, [lnk], [lnk], scale=-0.5)
                stt(yt[num, :n], yd[num, :n], dcol(li, DC_DF_SUB, num), ln[num, :n], ALU.mult, ALU.mult,
                    [ydk, lnk, "dcol"], [ytk])
                if hh == 1:
                    dma(ybr_d[ti, :, 1 * 2 + c, 0:n], yt[:, :n], [ytk], [("ybr", ti)])

            for ti in range(5):
                kts = full_kts if ti < 4 else ctx_kts
                for hh in range(2):
                    for m in range(2):
                        g = hh * 2 + m
                        hk = None
                        if m == 0 and pending:
                            hk = pending.pop(0)
                        attention(li, (lambda g: lambda t0_, n_: AR[:, g, t0_:t0_ + n_])(g),
                                  lambda kt: AR[:, 4, kt * 128:(kt + 1) * 128],
                                  VA, c, hh == 1, [(ti, kts, None)], df_out,
                                  (lambda g: lambda ti_: [skey(g, ti_)])(g),
                                  lambda kt: [skey(4, min(kt // 4, 4))], hook=hk)
                    (y0, y0k), (y1, y1k) = state.pop(ti)
                    pending.append((lambda ti, hh, y0, y0k, y1, y1k: lambda: df_post(ti, hh, y0, y0k, y1, y1k))(ti, hh, y0, y0k, y1, y1k))
            while pending:
                pending.pop(0)()
        mode_proj()
        wv, wvk = wload(li, "gq_v")

        def gq_vparts(tk0, ti):
            return [(H[:, kc, tk0:tk0 + 128], wv[:, kc * 128:(kc + 1) * 128]) for kc in range(8)], [wvk] + hkeys(ti)

        VA = v_project(gq_vparts, 128, 1)
        wk_, wkk = wload(li, "gq_k")
        project_norm(hparts(wk_, 128), 128, 0, M_O64, 1.0 / 64, lcol(li, C_GQ_GK),
                     [sdest(4) + (None,)], (M_PGQ, 1), ALL_TI, [wkk], hr)
        for ci, cn in enumerate(("qA", "qB")):
            wq, wqk = wload(li, f"gq_{cn}")
            project_norm(hparts(wq, 128), 128, 0, M_O64, 1.0 / 64, dcol(li, DC_GQ_GQ),
                         [sdest(ci * 2 + 0) + (kcol(K_MLO),), sdest(ci * 2 + 1) + (kcol(K_MHI),)],
                         (M_PGQ, 1), ALL_TI, [wqk], hr)
        mode_attn()
        for ci in range(2):
            ofn, flush = branch_out_writer(2, ci)
            for (ti, kts, pat0) in std_qtiles():
                for hh in range(2):
                    si = ci * 2 + hh
                    attention(li, (lambda si: lambda t0, n: AR[:, si, t0:t0 + n])(si),
                              lambda kt: AR[:, 4, kt * 128:(kt + 1) * 128],
                              VA, 0, hh == 1, [(ti, kts, None)], ofn,
                              (lambda si: lambda ti_: [skey(si, ti_)])(si),
                              lambda kt: [skey(4, min(kt // 4, 4))])
                flush(ti)
        mode_proj()
        w0, w0k = wload(li, "ml_qa0")
        w1, w1k = wload(li, "ml_qa1")
        for ti in ALL_TI:
            t0, n = TT[ti]
            pa, pak = PSm.get()
            for kc in range(8):
                mm(pa[:, :n], w0[:, kc * 128:(kc + 1) * 128], H[:, kc, t0:t0 + n], kc == 0, kc == 7, [w0k] + hkeys(ti), [pak])
            ra, rak = TF.get()
            act(ra[:, :n], pa[:, :n], AF.Copy, [pak], [rak])
            pb, pbk = PSm.get()
            for kc in range(8):
                mm(pb[0:64, :n], w1[:, kc * 64:(kc + 1) * 64], H[:, kc, t0:t0 + n], kc == 0, kc == 7, [w1k] + hkeys(ti), [pbk])
            rb, rbk = TF.get()
            act(rb[0:64, :n], pb[0:64, :n], AF.Copy, [pbk], [rbk])
            sqa, sqak = TB.get()
            tt("pool", sqa[:, :n], ra[:, :n], ra[:, :n], ALU.mult, [rak], [sqak])
            sqb, sqbk = TB.get()
            tt("pool", sqb[0:64, :n], rb[0:64, :n], rb[0:64, :n], ALU.mult, [rbk], [sqbk])
            st, stk = PSx.get()
            mm(st[:, :n], CMB[:, M_ONES, :], sqa[:, :n], True, False, [sqak, "cmb"], [stk])
            mm(st[:, :n], CMB[0:64, M_ONES, :], sqb[0:64, :n], False, True, [sqbk, "cmb"], [stk])
            ln, lnk = TF.get()
            act(ln[:, :n], st[:, :n], AF.Ln, [stk, "kcol"], [lnk], scale=1.0 / 192, bias=kcol(K_EPS))
            act(ln[:, :n], ln[:, :n], AF.Exp, [lnk], [lnk], scale=-0.5)
            stt(AR[:, 0, t0:t0 + n], ra[:, :n], lcol(li, C_ML_GQA0), ln[:, :n], ALU.mult, ALU.mult, [rak, lnk, "cols"], [skey(0, ti)])
            stt(AR[0:64, 1, t0:t0 + n], rb[0:64, :n], lcol(li, C_ML_GQA1, slice(0, 64)), ln[0:64, :n], ALU.mult, ALU.mult,
                [rbk, lnk, "cols"], [skey(1, ti)])
        wkv, wkvk = wload(li, "ml_kva")
        project_norm(hparts(wkv, 128), 128, 0, M_ONES, 1.0 / 128, None, [sdest(2) + (lcol(li, C_ML_GKVA),)], None,
                     ALL_TI, [wkvk], hr)
        wkr, wkrk = wload(li, "ml_kr")
        project_norm(hparts(wkr, 32), 32, 64, M_OML, 1.0 / 32, lcol(li, C_ML_GK, slice(64, 96)),
                     [sdest(3, slice(64, 96)) + (None,)], (M_PML, 2), ALL_TI, [wkrk], hr)
        w2, w2k = wload(li, "ml_2nd")
        Pipe.drain()

        def ml_vparts(tk0, ti):
            return [(AR[:, 2, tk0:tk0 + 128], w2[:, 1024:1280])], [w2k, skey(2, ti)]

        VA = v_project(ml_vparts, 256, 2)
        for c2 in range(2):
            ofn, flush = branch_out_writer(3, c2)
            qs, ks = 4, 5
            for hh in range(2):
                h = c2 * 2 + hh
                r96 = slice(0, 96)
                mode_proj()
                qparts = [(w2[:, h * 192:h * 192 + 96], lambda t0, n: AR[:, 0, t0:t0 + n]),
                          (w2[0:64, h * 192 + 96:h * 192 + 192], lambda t0, n: AR[0:64, 1, t0:t0 + n])]
                project_norm(qparts, 96, 0, M_OML, kcol(K_INVD_ML, r96), dcol(li, DC_ML_GQ, r96),
                             [sdest(qs, r96) + (None,)], (M_PML, 2), ALL_TI, [w2k],
                             lambda ti: [skey(0, ti), skey(1, ti)])
                kparts = [(w2[:, 768 + h * 64:768 + (h + 1) * 64], lambda t0, n: AR[:, 2, t0:t0 + n])]
                r64 = slice(0, 64)
                project_norm(kparts, 64, 0, M_OML, 1.0 / 64, None,
                             [sdest(ks, r64) + (lcol(li, C_ML_GK, r64),)], None, ALL_TI, [w2k],
                             lambda ti: [skey(2, ti)])
                mode_attn()
                for ti in ALL_TI:
                    t0, n = TT[ti]
                    op("pool", (lambda t0, n: lambda e: e.tensor_copy(out=AR[64:96, ks, t0:t0 + n], in_=AR[64:96, 3, t0:t0 + n]))(t0, n),
                       [skey(3, ti)], [skey(ks, ti)])
                attention(li, lambda t0, n: AR[0:96, qs, t0:t0 + n],
                          lambda kt: AR[0:96, ks, kt * 128:(kt + 1) * 128],
                          VA, c2, hh == 1, std_qtiles(), ofn,
                          lambda ti_: [skey(qs, ti_)],
                          lambda kt: [skey(ks, min(kt // 4, 4))])
            for ti in ALL_TI:
                flush(ti)
        mode_dense()
        for ti, (t0, n) in enumerate(TT):
            j = b if ti < 4 else 4
            yin = AR[:, 2 + 2 * (ti % 2):4 + 2 * (ti % 2), :].rearrange("p s t -> p (s t)")[:, 0:4096].rearrange("p (c t) -> p c t", t=512)
            yink = [skey(2 + 2 * (ti % 2), tq) for tq in range(5)] + [skey(3 + 2 * (ti % 2), tq) for tq in range(5)]
            dma(yin[:, :, :], ybr_d[ti, :, :, :], [("ybr", ti)], yink)
            ym = AR[:, 0:2, :].rearrange("p s t -> p (s t)")[:, 0:4096].rearrange("p (c t) -> p c t", t=512)
            ymk = [skey(0, tq) for tq in range(5)] + [skey(1, tq) for tq in range(5)]
            for oc in range(8):
                acc = None
                for i in range(4):
                    wt, wtk = wload(li, f"gb_{i}_{oc}")
                    pg, pgk = PSm.get()
                    for kc in range(8):
                        mm(pg[:, :n], wt[:, kc * 128:(kc + 1) * 128], H[:, kc, t0:t0 + n], kc == 0, kc == 7,
                           [wtk] + hkeys(ti), [pgk])
                    sg, sgk = TF.get()
                    act(sg[:, :n], pg[:, :n], AF.Sigmoid, [pgk], [sgk])
                    pb, pbk = PSm.get()
                    for c2 in range(2):
                        mm(pb[:, :n], wt[:, 1024 + c2 * 128:1024 + (c2 + 1) * 128], yin[:, i * 2 + c2, :n], c2 == 0, c2 == 1,
                           [wtk] + yink, [pbk])
                    if i == 0:
                        acc, acck = TF.get()
                        tt("dve", acc[:, :n], pb[:, :n], sg[:, :n], ALU.mult, [pbk, sgk], [acck])
                    else:
                        tt("dve", sg[:, :n], pb[:, :n], sg[:, :n], ALU.mult, [pbk, sgk], [sgk])
                        if i < 3:
                            tt("pool", acc[:, :n], acc[:, :n], sg[:, :n], ALU.add, [acck, sgk], [acck])
                        else:
                            tt("pool", ym[:, oc, :n], acc[:, :n], sg[:, :n], ALU.add, [acck, sgk], ymk)
            for oc2 in range(8):
                wt, wtk = wload(li, f"wo_{oc2}")
                po, pok = PSm.get()
                for oc in range(8):
                    mm(po[:, :n], wt[:, oc * 128:(oc + 1) * 128], ym[:, oc, :n], oc == 0, oc == 7, [wtk] + ymk, [pok])
                stt(X[:, oc2, t0:t0 + n], po[:, :n], modcol(li, 2, oc2, j), X[:, oc2, t0:t0 + n], ALU.mult, ALU.add,
                    [pok, ("modr", li), xkeys(oc2, ti)], [xkeys(oc2, ti)])
            if not skip_ffn and _os.environ.get("NORM2_INLOOP", "1") == "1":
                norm_phase(li, b, 1, tiles=(ti,))
        if skip_ffn:
            return
        if _os.environ.get("NORM2_INLOOP", "1") != "1":
            norm_phase(li, b, 1)
        U = AR[:, 0:4, :].rearrange("p s t -> p (s t)").bitcast(F32).rearrange("p (k t) -> p k t", k=2)
        ukeys = [[skey(2 * kd, tq) for tq in range(5)] + [skey(2 * kd + 1, tq) for tq in range(5)] for kd in range(2)]
        for g, js in enumerate(FF_GROUPS):
            for jj, jx in enumerate(js):
                ws = []
                for kd, nm in enumerate(("up_g", "up_v")):
                    ws.append(wload(li, f"{nm}_{jx}"))
                ccs = {}

                def taps(ti, jj=jj, jx=jx, ccs=ccs):
                    t0, n = TT[ti]
                    s0, s1 = (0, SEQ) if ti < 4 else (SEQ, T)
                    for kd in range(2):
                        base = C_CONV + (kd * NJ + jx) * 4
                        cc, cck = ccs[(kd, ti)]
                        lo = max(t0, s0 + 1)
                        stt(cc[:, lo - t0:n], U[:, kd, lo - 1:t0 + n - 1], lcol(li, base + 0), cc[:, lo - t0:n], ALU.mult, ALU.add,
                            ukeys[kd] + ["cols", cck], [cck])
                        hi = min(t0 + n, s1 - 1)
                        stt(cc[:, 0:hi - t0], U[:, kd, t0 + 1:hi + 1], lcol(li, base + 2), cc[:, 0:hi - t0], ALU.mult, ALU.add,
                            ukeys[kd] + ["cols", cck], [cck])
                    sg, sgk = TF.get()
                    act(sg[:, :n], ccs[(0, ti)][0][:, :n], AF.Silu, [ccs[(0, ti)][1]], [sgk])
                    tt("pool", AR[:, 4 + jj, t0:t0 + n], sg[:, :n], ccs[(1, ti)][0][:, :n], ALU.mult,
                       [sgk, ccs[(1, ti)][1]], [skey(4 + jj, ti)])

                for ti, (t0, n) in enumerate(TT):
                    for kd in range(2):
                        wt, wtk = ws[kd]
                        pu, puk = PSm.get()
                        for kc in range(8):
                            mm(pu[:, :n], wt[:, kc * 128:(kc + 1) * 128], H[:, kc, t0:t0 + n], kc == 0, kc == 7,
                               [wtk] + hkeys(ti), [puk])
                        base = C_CONV + (kd * NJ + jx) * 4
                        cc, cck = TF.get()
                        ccs[(kd, ti)] = (cc, cck)
                        act(U[:, kd, t0:t0 + n], pu[:, :n], AF.Copy, [puk], ukeys[kd])
                        if _os.environ.get("FFN_ACT_CENTER", "1") == "1":
                            act(cc[:, :n], pu[:, :n], AF.Identity, [puk, "cols"], [cck], scale=lcol(li, base + 1),
                                bias=lcol(li, base + 3))
                        else:
                            ts("dve", cc[:, :n], pu[:, :n], lcol(li, base + 1), lcol(li, base + 3), ALU.mult, ALU.add,
                               [puk, "cols"], [cck])
                    if ti >= 1:
                        taps(ti - 1)
                taps(4)
            for oc in range(8):
                wt, wtk = wload(li, f"dn_{g}_{oc}")
                for ti, (t0, n) in enumerate(TT):
                    j = b if ti < 4 else 4
                    pd, pdk = PSm.get()
                    for jj in range(len(js)):
                        mm(pd[:, :n], wt[:, jj * 128:(jj + 1) * 128], AR[:, 4 + jj, t0:t0 + n], jj == 0, jj == len(js) - 1,
                           [wtk, skey(4 + jj, ti)], [pdk])
                    stt(X[:, oc, t0:t0 + n], pd[:, :n], modcol(li, 5, oc, j), X[:, oc, t0:t0 + n], ALU.mult, ALU.add,
                        [pdk, ("modr", li), xkeys(oc, ti)], [xkeys(oc, ti)])

    fin = []
    for b in range(NB):
        for c in range(8):
            dma(X[:, c, :], xT_d[b, :, c, :], [], [xkeys(c, ti) for ti in range(5)])
        for li in range(L):
            layer(li, b)
        for c in range(8):
            fin.append(dma(yT_d[b, :, c, :], X[:, c, :], [xkeys(c, ti) for ti in range(5)], []))
    S.emit(final_wait_ops=fin)
    S.close()
    return nc


def prepare_shared(layers, w_mod, b_mod, norm1_g, norm2_g, w_in, na_qk_g, na_rel_bias, diff_qk_g, diff_lambda,
                   diff_subln_g, gqa_qk_g, mla_qa_g, mla_kva_g, w_mla_qb, w_mla_kvb, mla_qk_g, w_gate, w_branch,
                   w_out, w_up, conv_w, conv_b, w_down):
    f = lambda a: np.asarray(a, dtype=np.float32)
    (w_mod, b_mod, norm1_g, norm2_g, w_in, na_qk_g, na_rel_bias, diff_qk_g, diff_lambda, diff_subln_g, gqa_qk_g,
     mla_qa_g, mla_kva_g, w_mla_qb, w_mla_kvb, mla_qk_g, w_gate, w_branch, w_out, w_up, conv_w, conv_b, w_down) = map(
        f, (w_mod, b_mod, norm1_g, norm2_g, w_in, na_qk_g, na_rel_bias, diff_qk_g, diff_lambda, diff_subln_g, gqa_qk_g,
            mla_qa_g, mla_kva_g, w_mla_qb, w_mla_kvb, mla_qk_g, w_gate, w_branch, w_out, w_up, conv_w, conv_b, w_down))
    L = len(layers)
    sh = {}
    wm = np.stack([w_mod[l] for l in layers])
    wm = wm.reshape(L, 8, 128, 48, 128).transpose(0, 3, 2, 1, 4).reshape(L, 48, 128, 1024)
    sh["wmod"] = np.ascontiguousarray(wm)
    bm = np.stack([b_mod[l] for l in layers])
    sh["bmod"] = np.ascontiguousarray(bm.reshape(L, 48, 128).transpose(2, 0, 1))
    sh["wblob"] = np.stack([_build_blob(l, w_in, w_mla_qb, w_mla_kvb, w_gate, w_branch, w_out, w_up, w_down)
                            for l in layers])
    cols = np.stack([_build_cols(l, norm1_g, norm2_g, na_qk_g, diff_qk_g, diff_subln_g, gqa_qk_g, mla_qa_g,
                                 mla_kva_g, mla_qk_g, conv_w, conv_b) for l in layers])
    sh["cols"] = np.ascontiguousarray(cols.transpose(1, 0, 2))
    sh["kcol"] = _const_cols()
    sh["cmat"] = _const_mats()
    sh["rope"] = _rope_tables()
    sh["nab"] = np.ascontiguousarray(_na_bias_tables(na_rel_bias)[layers])
    sh["narm"] = _na_row_masks()
    lam = np.stack([diff_lambda[l].reshape(128) for l in layers])
    sh["lam"] = np.ascontiguousarray(np.broadcast_to(lam[None], (128, L, 128)))
    return sh


def lambda_init_of(i):
    return 0.8 - 0.6 * math.exp(-0.3 * i)


def run_layers(x, ctx, c, c_ctx, layers, weights, ncores=NCORES, debug=False, skip_ffn=False):
    B = x.shape[0]
    NB = B // ncores
    sh = prepare_shared(layers, **weights)
    lam = [lambda_init_of(l) for l in layers]
    nc = build_program(NB, layers=list(layers), lambda_inits=lam, debug=debug, skip_ffn=skip_ffn)
    in_maps = []
    for k in range(ncores):
        xs = np.concatenate([x[k * NB:(k + 1) * NB], ctx[k * NB:(k + 1) * NB]], axis=1)
        xT = np.ascontiguousarray(xs.reshape(NB, T, 8, 128).transpose(0, 3, 2, 1))
        cv = np.concatenate([c[k * NB:(k + 1) * NB], c_ctx[None, :]], axis=0)
        if NB < 4:
            cv = np.concatenate([cv[:NB], np.zeros((4 - NB, D), np.float32), cv[NB:]], axis=0)
        cT = np.ascontiguousarray(cv.reshape(5, 8, 128).transpose(2, 1, 0))
        m = dict(sh)
        m["xT"] = xT
        m["cT"] = cT
        in_maps.append(m)
    res = run_bass_kernel_spmd(nc, in_maps, core_ids=list(range(ncores)))
    outs = []
    for k in range(ncores):
        yT = res.results[k]["yT"]
        outs.append(yT.transpose(0, 3, 2, 1).reshape(NB, T, D))
    o = np.concatenate(outs, axis=0)
    if debug:
        return o[:, :SEQ], o[:, SEQ:], res.results[0]["ybr"]
    return o[:, :SEQ], o[:, SEQ:]


def kernel(x, c, ctx, c_ctx, w_mod, b_mod, norm1_g, norm2_g, w_in, na_qk_g, na_rel_bias,
           diff_qk_g, diff_lambda, diff_subln_g, gqa_qk_g, mla_qa_g, mla_kva_g, w_mla_qb, w_mla_kvb,
           mla_qk_g, w_gate, w_branch, w_out, w_up, conv_w, conv_b, w_down):
    weights = dict(w_mod=w_mod, b_mod=b_mod, norm1_g=norm1_g, norm2_g=norm2_g, w_in=w_in, na_qk_g=na_qk_g,
                   na_rel_bias=na_rel_bias, diff_qk_g=diff_qk_g, diff_lambda=diff_lambda, diff_subln_g=diff_subln_g,
                   gqa_qk_g=gqa_qk_g, mla_qa_g=mla_qa_g, mla_kva_g=mla_kva_g, w_mla_qb=w_mla_qb, w_mla_kvb=w_mla_kvb,
                   mla_qk_g=mla_qk_g, w_gate=w_gate, w_branch=w_branch, w_out=w_out, w_up=w_up, conv_w=conv_w,
                   conv_b=conv_b, w_down=w_down)
    x = np.asarray(x, np.float32)
    ctx = np.asarray(ctx, np.float32)
    c = np.asarray(c, np.float32)
    c_ctx = np.asarray(c_ctx, np.float32)
    out, _ = run_layers(x, ctx, c, c_ctx, list(range(DEPTH)), weights)
    return np.ascontiguousarray(out, dtype=np.float32)
```

```python
import math
from contextlib import ExitStack
import numpy as np
import concourse.bass as bass
import concourse.mybir as mybir
from concourse.bass_utils import run_bass_kernel_spmd

F32 = mybir.dt.float32
BF16 = mybir.dt.bfloat16
AF = mybir.ActivationFunctionType
ALU = mybir.AluOpType
AX = mybir.AxisListType

D = 1024
SEQ = 2048
CTX = 256
T = SEQ + CTX
DEPTH = 4
NCORES = 8
GRID_W = 64
EPS = 1e-6
THETA = 10000.0
D_FF = 2816
NJ = D_FF // 128
IN_W = 2400
O_NA_Q, O_NA_K, O_NA_V = 0, 256, 512
O_DF_Q, O_DF_K, O_DF_V = 768, 1024, 1280
O_GQ_Q, O_GQ_K, O_GQ_V = 1536, 1792, 1920
O_ML_QA, O_ML_KVA, O_ML_KR = 2048, 2240, 2368
TT = [(0, 512), (512, 512), (1024, 512), (1536, 512), (2048, 256)]
FF_GROUPS = [list(range(0, 5)), list(range(5, 10)), list(range(10, 14)), list(range(14, 18)), list(range(18, 22))]
NEG = -30000.0

C_G1, C_G2 = 0, 8
C_NA_GQ, C_NA_GK, C_DF_GQ, C_DF_GK, C_GQ_GQ, C_GQ_GK = 16, 17, 18, 19, 20, 21
C_ML_GQA0, C_ML_GQA1, C_ML_GKVA, C_ML_GQ, C_ML_GK, C_DF_SUB = 22, 23, 24, 25, 26, 27
C_CONV = 28
NCOL = C_CONV + 2 * NJ * 4
K_MLO, K_MHI, K_DM0, K_INVD_ML, K_EPS = 0, 1, 2, 6, 7
NKC = 8
DC_NA_LO, DC_NA_HI, DC_DF_GQ, DC_GQ_GQ, DC_ML_GQ, DC_DF_SUB, DC_NEGLAM = 0, 1, 2, 3, 4, 5, 6
NDC = 8
M_ONES, M_O64, M_O32, M_OML, M_PDF, M_PGQ, M_PML = 0, 1, 2, 3, 0, 1, 2
NMAT = 7
NA_KT = {0: list(range(0, 6)), 1: list(range(2, 10)), 2: list(range(6, 14)), 3: list(range(10, 16))}


def _weight_layout():
    names = []

    def add(n, c):
        names.append((n, c))

    for br in ("na", "df"):
        for n in ("q0", "q1", "k0", "k1", "va", "vb"):
            add(f"{br}_{n}", 1024)
    for n in ("qA", "qB", "k", "v"):
        add(f"gq_{n}", 1024)
    add("ml_qa0", 1024)
    add("ml_qa1", 512)
    add("ml_kva", 1024)
    add("ml_kr", 256)
    add("ml_2nd", 1280)
    for oc in range(8):
        for i in range(4):
            add(f"gb_{i}_{oc}", 1280)
    for oc in range(8):
        add(f"wo_{oc}", 1024)
    for j in range(NJ):
        add(f"up_g_{j}", 1024)
        add(f"up_v_{j}", 1024)
    for g, js in enumerate(FF_GROUPS):
        for oc in range(8):
            add(f"dn_{g}_{oc}", len(js) * 128)
    off = {}
    o = 0
    for n, c in names:
        off[n] = (o, c)
        o += c
    return names, off, o


W_NAMES, W_OFF, WCOLS = _weight_layout()
CAST_PIECE = 8192
N_PIECES = (WCOLS + CAST_PIECE - 1) // CAST_PIECE


def _tilew(W):
    K, M = W.shape
    nk = (K + 127) // 128
    if nk * 128 != K:
        Wp = np.zeros((nk * 128, M), np.float32)
        Wp[:K] = W
    else:
        Wp = W
    return Wp.reshape(nk, 128, M).transpose(1, 0, 2).reshape(128, nk * M)


def _build_blob(l, w_in, w_mla_qb, w_mla_kvb, w_gate, w_branch, w_out, w_up, w_down):
    blob = np.empty((128, WCOLS), np.float32)

    def put(name, arr):
        o, c = W_OFF[name]
        assert arr.shape == (128, c), (name, arr.shape, c)
        blob[:, o:o + c] = arr

    wi = w_in[l]
    for br, oq, ok, ov in (("na", O_NA_Q, O_NA_K, O_NA_V), ("df", O_DF_Q, O_DF_K, O_DF_V)):
        put(f"{br}_q0", _tilew(wi[:, oq:oq + 128]))
        put(f"{br}_q1", _tilew(wi[:, oq + 128:oq + 256]))
        put(f"{br}_k0", _tilew(wi[:, ok:ok + 128]))
        put(f"{br}_k1", _tilew(wi[:, ok + 128:ok + 256]))
        put(f"{br}_va", _tilew(wi[0:512, ov:ov + 256]))
        put(f"{br}_vb", _tilew(wi[512:1024, ov:ov + 256]))
    qa = np.concatenate([wi[:, O_GQ_Q:O_GQ_Q + 64], wi[:, O_GQ_Q + 128:O_GQ_Q + 192]], axis=1)
    qb = np.concatenate([wi[:, O_GQ_Q + 64:O_GQ_Q + 128], wi[:, O_GQ_Q + 192:O_GQ_Q + 256]], axis=1)
    put("gq_qA", _tilew(qa))
    put("gq_qB", _tilew(qb))
    put("gq_k", _tilew(wi[:, O_GQ_K:O_GQ_K + 128]))
    put("gq_v", _tilew(wi[:, O_GQ_V:O_GQ_V + 128]))
    put("ml_qa0", _tilew(wi[:, O_ML_QA:O_ML_QA + 128]))
    put("ml_qa1", _tilew(wi[:, O_ML_QA + 128:O_ML_QA + 192]))
    put("ml_kva", _tilew(wi[:, O_ML_KVA:O_ML_KVA + 128]))
    put("ml_kr", _tilew(wi[:, O_ML_KR:O_ML_KR + 32]))
    parts = []
    for h in range(4):
        parts.append(_tilew(w_mla_qb[l][:, h * 96:(h + 1) * 96]))
    for h in range(4):
        parts.append(_tilew(w_mla_kvb[l][:, h * 128:h * 128 + 64]))
    parts.append(_tilew(np.concatenate([w_mla_kvb[l][:, h * 128 + 64:h * 128 + 128] for h in range(4)], axis=1)))
    put("ml_2nd", np.concatenate(parts, axis=1))
    gq_perm = np.concatenate([np.arange(0, 64), np.arange(128, 192), np.arange(64, 128), np.arange(192, 256)])
    for i in range(4):
        wb = w_branch[l, i]
        if i == 2:
            wb = wb[gq_perm]
        for oc in range(8):
            g = _tilew(w_gate[l, i][:, oc * 128:(oc + 1) * 128])
            b = _tilew(wb[:, oc * 128:(oc + 1) * 128])
            put(f"gb_{i}_{oc}", np.concatenate([g, b], axis=1))
    for oc in range(8):
        put(f"wo_{oc}", _tilew(w_out[l][:, oc * 128:(oc + 1) * 128]))
    for j in range(NJ):
        put(f"up_g_{j}", _tilew(w_up[l][:, j * 128:(j + 1) * 128]))
        put(f"up_v_{j}", _tilew(w_up[l][:, D_FF + j * 128:D_FF + (j + 1) * 128]))
    for g, js in enumerate(FF_GROUPS):
        rows = w_down[l][js[0] * 128:(js[-1] + 1) * 128]
        for oc in range(8):
            put(f"dn_{g}_{oc}", _tilew(rows[:, oc * 128:(oc + 1) * 128]))
    return blob


def _build_cols(l, norm1_g, norm2_g, na_qk_g, diff_qk_g, diff_subln_g, gqa_qk_g, mla_qa_g, mla_kva_g,
                mla_qk_g, conv_w, conv_b):
    c = np.zeros((128, NCOL), np.float32)
    c[:, C_G1:C_G1 + 8] = norm1_g[l].reshape(8, 128).T
    c[:, C_G2:C_G2 + 8] = norm2_g[l].reshape(8, 128).T
    c[:, C_NA_GQ] = np.tile(na_qk_g[l, 0], 2)
    c[:, C_NA_GK] = np.tile(na_qk_g[l, 1], 2)
    c[:, C_DF_GQ] = np.tile(diff_qk_g[l, 0], 4)
    c[:, C_DF_GK] = np.tile(diff_qk_g[l, 1], 4)
    c[:, C_GQ_GQ] = np.tile(gqa_qk_g[l, 0], 2)
    c[:, C_GQ_GK] = np.tile(gqa_qk_g[l, 1], 2)
    c[:, C_ML_GQA0] = mla_qa_g[l, 0:128]
    c[0:64, C_ML_GQA1] = mla_qa_g[l, 128:192]
    c[:, C_ML_GKVA] = mla_kva_g[l]
    c[0:96, C_ML_GQ] = mla_qk_g[l, 0]
    c[0:96, C_ML_GK] = mla_qk_g[l, 1]
    c[:, C_DF_SUB] = np.tile(diff_subln_g[l], 2)
    for kind in range(2):
        for j in range(NJ):
            lo = kind * D_FF + j * 128
            base = C_CONV + (kind * NJ + j) * 4
            for tap in range(3):
                c[:, base + tap] = conv_w[l, tap, lo:lo + 128]
            c[:, base + 3] = conv_b[l, lo:lo + 128]
    return c


def _const_cols():
    k = np.zeros((128, NKC), np.float32)
    p = np.arange(128)
    k[:, K_MLO] = (p < 64)
    k[:, K_MHI] = (p >= 64)
    for g in range(4):
        k[:, K_DM0 + g] = (p // 32 == g)
    k[0:64, K_INVD_ML] = 1.0 / 64
    k[64:128, K_INVD_ML] = 1.0 / 32
    k[:, K_EPS] = EPS
    return k


def _const_mats():
    m = np.zeros((NMAT, 128, 128), np.float32)
    p = np.arange(128)
    m[M_ONES] = 1.0
    m[M_O64] = (p[:, None] // 64 == p[None, :] // 64)
    m[M_O32] = (p[:, None] // 32 == p[None, :] // 32)
    blk = np.where(p < 64, 0, np.where(p < 96, 1, 2 + p))
    m[M_OML] = (blk[:, None] == blk[None, :])

    def perm(G, lo, hi):
        a = np.zeros((128, 128), np.float32)
        h = G // 2
        for mm_ in range(lo, hi):
            loc = (mm_ - lo) % G
            if loc < h:
                a[mm_ + h, mm_] = -1.0
            else:
                a[mm_ - h, mm_] = 1.0
        return a

    m[4 + M_PDF] = perm(32, 0, 128)
    m[4 + M_PGQ] = perm(64, 0, 128)
    m[4 + M_PML] = perm(32, 64, 96)
    return np.ascontiguousarray(m.transpose(1, 0, 2))


def _rope_tables():
    t = np.arange(SEQ)
    row = (t // GRID_W).astype(np.float32)
    col = (t % GRID_W).astype(np.float32)

    def ang(rot_dim):
        n_axis = rot_dim // 4
        inv = (np.float32(THETA) ** (-np.arange(n_axis, dtype=np.float32) / np.float32(n_axis))).astype(np.float32)
        return np.concatenate([row[:, None] * inv, col[:, None] * inv], axis=-1).astype(np.float32)

    tab = np.zeros((3, 2, 128, SEQ), np.float32)
    p = np.arange(128)
    a = ang(32)
    tab[0, 0] = np.cos(a)[:, p % 16].T
    tab[0, 1] = np.sin(a)[:, p % 16].T
    a = ang(64)
    tab[1, 0] = np.cos(a)[:, p % 32].T
    tab[1, 1] = np.sin(a)[:, p % 32].T
    a = ang(32)
    tab[2, 0] = 1.0
    tab[2, 0, 64:96] = np.cos(a)[:, (p[64:96] - 64) % 16].T
    tab[2, 1, 64:96] = np.sin(a)[:, (p[64:96] - 64) % 16].T
    return tab


NA_PATS = [(j, kt) for j in range(4) for kt in NA_KT[j]]
NA_PAT_IDX = {jk: i for i, jk in enumerate(NA_PATS)}
NA_NF = 22


def _na_bias_tables(na_rel_bias):
    L = na_rel_bias.shape[0]
    p = np.arange(128)
    kl = p // 64
    kc = p % 64
    fp = np.arange(NA_NF)
    qc = np.arange(64)
    dr = kl[:, None] + 10 - fp[None, :]
    cs = np.clip(qc - 8, 0, 48)
    colv = (kc[:, None] >= cs[None, :]) & (kc[:, None] < cs[None, :] + 16)
    valid = ((dr >= -7) & (dr <= 7))[:, :, None] & colv[:, None, :]
    ri = np.clip(dr + 7, 0, 14)[:, :, None] + np.zeros((1, 1, 64), np.int64)
    ci = (np.clip(kc[:, None] - qc[None, :], -15, 15) + 15)[:, None, :] + np.zeros((1, NA_NF, 1), np.int64)
    g = na_rel_bias[:, :, ri, ci]
    return np.ascontiguousarray(np.where(valid[None, None], g, np.float32(NEG)).astype(np.float32))


def _na_row_masks():
    rm = np.zeros((128, len(NA_PATS), 8), np.float32)
    p = np.arange(128)
    for i, (j, kt) in enumerate(NA_PATS):
        kr = 2 * kt + p // 64
        qr = 8 * j + np.arange(8)
        rs = np.clip(qr - 4, 0, 24)
        ok = (kr[:, None] >= rs[None, :]) & (kr[:, None] < rs[None, :] + 8)
        rm[:, i, :] = np.where(ok, 0.0, NEG)
    return rm


ENGS = ("pe", "act", "dve", "pool", "sp")
N_DMA_SEMS = 40
SEM_WRAP = 30000


class Sched:
    def __init__(self, nc):
        self.nc = nc
        self.ops = []
        self.last_w = {}
        self.readers = {}
        self.stack = ExitStack()

    def sb(self, name, shape, dt):
        return self.stack.enter_context(self.nc.sbuf_tensor(name, list(shape), dt))

    def ps(self, name, shape, dt=F32):
        return self.stack.enter_context(self.nc.psum_tensor(name, list(shape), dt))

    def op(self, eng, fn, reads=(), writes=(), dma=False):
        i = len(self.ops)
        deps = set()
        for k in reads:
            w = self.last_w.get(k)
            if w is not None:
                deps.add(w)
            if type(k) is tuple and k[0] == "ps":
                for r in self.readers.get(k, ()):
                    if self.ops[r][0] != eng:
                        deps.add(r)
        for k in writes:
            w = self.last_w.get(k)
            if w is not None:
                deps.add(w)
            r = self.readers.get(k)
            if r:
                deps.update(r)
        for k in reads:
            self.readers.setdefault(k, []).append(i)
        for k in writes:
            self.last_w[k] = i
            self.readers[k] = []
        ops = self.ops
        deps = [d for d in deps if dma or ops[d][3] or ops[d][0] != eng or eng != "pe"]
        ops.append([eng, fn, deps, dma, False, None])
        return i

    def emit(self, final_wait_ops=()):
        nc = self.nc
        ops = self.ops
        for o in ops:
            for d in o[2]:
                ops[d][4] = True
        for i in final_wait_ops:
            ops[i][4] = True
        cnt = {e: 0 for e in ENGS}
        nsem = {e: 1 for e in ENGS}
        dma_k = [0] * N_DMA_SEMS
        nd = 0
        for o in ops:
            if o[3]:
                s = nd % N_DMA_SEMS
                nd += 1
                dma_k[s] += 1
                o[5] = ("d", s, 16 * dma_k[s])
            elif o[4]:
                e = o[0]
                cnt[e] += 1
                ep, c = divmod(cnt[e] - 1, SEM_WRAP)
                o[5] = (e, ep, c + 1)
                nsem[e] = max(nsem[e], ep + 1)
        sems = {}
        for e in ENGS:
            for ep in range(nsem[e]):
                sems[(e, ep)] = self.stack.enter_context(nc.semaphore(f"s_{e}_{ep}"))
        for s in range(min(N_DMA_SEMS, max(nd, 1))):
            sems[("d", s)] = self.stack.enter_context(nc.semaphore(f"s_dma_{s}"))
        block = self.stack.enter_context(nc.Block())
        by_eng = {e: [] for e in ENGS}
        for i, o in enumerate(ops):
            by_eng[o[0]].append(i)

        def make(e):
            def body(eng):
                waited = {}
                for i in by_eng[e]:
                    o = ops[i]
                    need = {}
                    for d in o[2]:
                        t = ops[d][5]
                        key = (t[0], t[1])
                        if need.get(key, 0) < t[2]:
                            need[key] = t[2]
                    if o[3]:
                        t = o[5]
                        if t[2] > 16:
                            key = (t[0], t[1])
                            if need.get(key, 0) < t[2] - 16:
                                need[key] = t[2] - 16
                    for key, v in need.items():
                        if waited.get(key, 0) < v:
                            eng.wait_ge(sems[key], v)
                            waited[key] = v
                    ins = o[1](eng)
                    if o[3]:
                        t = o[5]
                        ins.then_inc(sems[(t[0], t[1])], 16)
                    elif o[4]:
                        t = o[5]
                        ins.then_inc(sems[(t[0], t[1])], 1)
                if e == "sp":
                    for i in final_wait_ops:
                        t = ops[i][5]
                        eng.wait_ge(sems[(t[0], t[1])], t[2])
            return body

        block.tensor(make("pe"))
        block.scalar(make("act"))
        block.vector(make("dve"))
        block.gpsimd(make("pool"))
        block.sync(make("sp"))

    def close(self):
        self.stack.close()


class Ring:
    def __init__(self, items, key, keys=None):
        self.items = items
        self.keys = keys if keys is not None else [(key, k) for k in range(len(items))]
        self.i = 0

    def get(self):
        k = self.i % len(self.items)
        self.i += 1
        return self.items[k], self.keys[k]


def build_program(NB, layers, lambda_inits, debug=False, skip_ffn=False):
    L = len(layers)
    nc = bass.Bass("TRN2", target_bir_lowering=False)
    S = Sched(nc)

    def dram_in(name, shape, dt=F32):
        return nc.dram_tensor(name, list(shape), dt, kind="ExternalInput").ap()

    xT_d = dram_in("xT", [NB, 128, 8, T])
    cT_d = dram_in("cT", [128, 8, 5])
    wmod_d = dram_in("wmod", [L, 48, 128, 1024])
    bmod_d = dram_in("bmod", [128, L, 48])
    wblob_d = dram_in("wblob", [L, 128, WCOLS])
    cols_d = dram_in("cols", [128, L, NCOL])
    kcol_d = dram_in("kcol", [128, NKC])
    cmat_d = dram_in("cmat", [128, NMAT, 128])
    rope_d = dram_in("rope", [3, 2, 128, SEQ])
    nab_d = dram_in("nab", [L, 4, 128, NA_NF * 64])
    narm_d = dram_in("narm", [128, len(NA_PATS) * 8])
    lam_d = dram_in("lam", [128, L, 128])
    yT_d = nc.dram_tensor("yT", [NB, 128, 8, T], F32, kind="ExternalOutput").ap()
    wbf_d = nc.dram_tensor("wbf", [L, 128, WCOLS], BF16, kind="Internal").ap()
    ybr_d = nc.dram_tensor("ybr", [5, 128, 8, 512], BF16, kind=("ExternalOutput" if debug else "Internal")).ap()

    X = S.sb("X", [128, 8, T], F32)
    H = S.sb("H", [128, 8, T], BF16)
    NSLOT = 9
    AR = S.sb("AR", [128, NSLOT, T], BF16)
    NWR = 4
    WRC = 1280
    WR = S.sb("WR", [128, NWR, WRC], BF16)
    NTF = 8
    TFt = S.sb("TF", [128, NTF, 512], F32)
    NTB = 5
    TBt = S.sb("TB", [128, NTB, 512], BF16)
    RTt = S.sb("RT", [128, 1024 + NA_NF * 64], F32)
    RMt = S.sb("RMT", [128, len(NA_PATS) * 8], F32)
    COLS = S.sb("COLS", [128, L, NCOL], F32)
    KCOL = S.sb("KCOL", [128, NKC], F32)
    DCOL = S.sb("DCOL", [128, L, NDC], F32)
    CMF = S.sb("CMF", [128, 3, 128], F32)
    CMB = S.sb("CMB", [128, 4, 128], BF16)
    MODR = S.sb("MODR", [128, L, 48, 5], F32)
    AMOD = S.sb("AMOD", [128, L, 2, 8, 5], F32)
    BMOD = S.sb("BMOD", [128, L, 48], F32)
    CT = S.sb("CT", [128, 8, 5], F32)
    LTMP = S.sb("LTMP", [128, 8], F32)
    YST = S.sb("YST", [128, 5, 512], BF16)

    PS = [S.ps(f"ps{i}", [128, 512]) for i in range(8)]
    def psring(idx):
        return Ring([PS[i] for i in idx], "ps", [("ps", i) for i in idx])

    PSm_small = psring([0, 1])
    import os as _os
    PSm_big = psring([0, 1, 3, 4, 5, 6, 7]) if _os.environ.get("PSM_BIG", "1") == "1" else PSm_small

    class _PSm:
        cur = PSm_big

        @staticmethod
        def get():
            return _PSm.cur.get()

    PSm = _PSm
    PSx_small = psring([2])
    PSx_big = psring([2, 4, 5, 6, 7])
    PSm_proj = psring([0, 1, 3])

    class _PSx:
        cur = PSx_small

        @staticmethod
        def get():
            return _PSx.cur.get()

    PSx = _PSx
    PSS = psring([3, 4, 5])
    PSo = psring([6, 7])

    def mode_proj():
        PSm.cur = PSm_proj
        PSx.cur = PSx_big

    def mode_attn():
        Pipe.drain()
        PSm.cur = PSm_small
        PSx.cur = PSx_small

    def mode_dense():
        Pipe.drain()
        PSm.cur = PSm_big
        PSx.cur = PSx_small
    TF = Ring([TFt[:, i, :] for i in range(NTF)], "tf")
    TB = Ring([TBt[:, i, :] for i in range(NTB)], "tb")
    RT = Ring([RTt[:, i * 1024:(i + 1) * 1024].rearrange("p (a b) -> p a b", a=2) for i in range(2)], "rt")
    NAT = RTt[:, 1024:1024 + NA_NF * 64]
    NATK = ("rt", 1)
    RSD = RTt[:, 512:1024]
    WRr = Ring([WR[:, i, :] for i in range(NWR)], "wr")

    op = S.op
    dmaq = ["sp"]

    def mm(out, lhsT, rhs, start, stop, reads, writes):
        return op("pe", lambda e: e.matmul(out, lhsT=lhsT, rhs=rhs, start=start, stop=stop), reads, writes)

    def act(out, in_, func, reads, writes, scale=1.0, bias=0.0):
        return op("act", lambda e: e.activation(out=out, in_=in_, func=func, bias=bias, scale=scale), reads, writes)

    def tt(eng, out, in0, in1, alu, reads, writes):
        return op(eng, lambda e: e.tensor_tensor(out=out, in0=in0, in1=in1, op=alu), reads, writes)

    def ts(eng, out, in0, s1, s2, op0, op1, reads, writes):
        if s2 is None:
            return op(eng, lambda e: e.tensor_scalar(out=out, in0=in0, scalar1=s1, scalar2=0.0, op0=op0, op1=ALU.add),
                      reads, writes)
        return op(eng, lambda e: e.tensor_scalar(out=out, in0=in0, scalar1=s1, scalar2=s2, op0=op0, op1=op1), reads, writes)

    def stt(out, in0, scalar, in1, op0, op1, reads, writes):
        return op("dve", lambda e: e.scalar_tensor_tensor(out=out, in0=in0, scalar=scalar, in1=in1, op0=op0, op1=op1),
                  reads, writes)

    def dma(out, in_, reads, writes, eng="sp"):
        return op(eng, lambda e: e.dma_start(out=out, in_=in_), reads, writes, dma=True)

    def kcol(i, rows=slice(0, 128)):
        return KCOL[rows, i:i + 1]

    def lcol(li, i, rows=slice(0, 128)):
        return COLS[rows, li, i:i + 1]

    def dcol(li, i, rows=slice(0, 128)):
        return DCOL[rows, li, i:i + 1]

    dma(COLS[:], cols_d, [], ["cols"])
    dma(KCOL[:], kcol_d, [], ["kcol"])
    dma(CMF[:], cmat_d[:, 4:7, :], [], ["cmf"])
    dma(CT[:], cT_d, [], ["ct"])
    dma(BMOD[:], bmod_d, [], ["bmod"])
    dma(RMt[:], narm_d, [], ["rmt"])
    op("pool", lambda e: e.memset(TFt[:], 0.0), [], [("tf", i) for i in range(NTF)])
    op("pool", lambda e: e.memset(TBt[:], 0.0), [], [("tb", i) for i in range(NTB)])
    op("pool", lambda e: e.memset(RTt[:], 0.0), [], [("rt", 0), ("rt", 1)])
    ones_st, onk = TF.get()
    dma(ones_st[:, 0:512], cmat_d[:, 0:4, :].rearrange("p a b -> p (a b)"), [onk], [onk])
    op("dve", lambda e: e.tensor_copy(out=CMB[:].rearrange("p a b -> p (a b)"), in_=ones_st[:, 0:512]), [onk], ["cmb"])
    LAMt, lamk = TF.get()
    dma(LAMt[:, 0:L * 128], lam_d.rearrange("p l k -> p (l k)"), [lamk], [lamk])
    LAM = LAMt[:, 0:L * 128].rearrange("p (l k) -> p l k", k=128)
    act(CT[:], CT[:], AF.Silu, ["ct"], ["ct"])
    for li in range(L):
        lam_init = lambda_inits[li]
        ts("dve", dcol(li, DC_NA_LO), lcol(li, C_NA_GQ), 0.125, kcol(K_MLO), ALU.mult, ALU.mult, ["cols", "kcol"], ["dcol"])
        ts("dve", dcol(li, DC_NA_HI), lcol(li, C_NA_GQ), 0.125, kcol(K_MHI), ALU.mult, ALU.mult, ["cols", "kcol"], ["dcol"])
        ts("dve", dcol(li, DC_DF_GQ), lcol(li, C_DF_GQ), 32.0 ** -0.5, None, ALU.mult, None, ["cols"], ["dcol"])
        ts("dve", dcol(li, DC_GQ_GQ), lcol(li, C_GQ_GQ), 0.125, None, ALU.mult, None, ["cols"], ["dcol"])
        ts("dve", dcol(li, DC_ML_GQ), lcol(li, C_ML_GQ), 96.0 ** -0.5, None, ALU.mult, None, ["cols"], ["dcol"])
        ts("dve", dcol(li, DC_DF_SUB), lcol(li, C_DF_SUB), 1.0 - lam_init, None, ALU.mult, None, ["cols"], ["dcol"])
        tt("dve", LAM[:, li, 0:32], LAM[:, li, 0:32], LAM[:, li, 32:64], ALU.mult, [lamk], [lamk])
        tt("dve", LAM[:, li, 64:96], LAM[:, li, 64:96], LAM[:, li, 96:128], ALU.mult, [lamk], [lamk])
        op("dve", lambda e, li=li: e.tensor_reduce(out=LTMP[:, 0:1], in_=LAM[:, li, 0:32], axis=AX.X, op=ALU.add),
           [lamk], ["ltmp"])
        op("dve", lambda e, li=li: e.tensor_reduce(out=LTMP[:, 1:2], in_=LAM[:, li, 64:96], axis=AX.X, op=ALU.add),
           [lamk], ["ltmp"])
        act(LTMP[:, 2:4], LTMP[:, 0:2], AF.Exp, ["ltmp"], ["ltmp"])
        tt("dve", LTMP[:, 4:5], LTMP[:, 3:4], LTMP[:, 2:3], ALU.subtract, ["ltmp"], ["ltmp"])
        ts("dve", dcol(li, DC_NEGLAM), LTMP[:, 4:5], -lam_init, None, ALU.add, None, ["ltmp"], ["dcol"])
    def emit_cast(li):
        for pc in range(N_PIECES):
            c0 = pc * CAST_PIECE
            c1 = min(WCOLS, c0 + CAST_PIECE)
            dma(wbf_d[li, :, c0:c1], wblob_d[li, :, c0:c1], [], [("wbf", li, pc)], eng="pool")

    emit_cast(0)
    def emit_mod(li):
        pm, pmk = PSx.get()
        for oc in range(48):
            wt, wk = TF.get()
            wt2, wk2 = TF.get()
            dma(wt[:, :], wmod_d[li, oc, :, 0:512], [], [wk])
            dma(wt2[:, :], wmod_d[li, oc, :, 512:1024], [], [wk2])
            for kc in range(8):
                src = wt if kc < 4 else wt2
                kk = kc % 4
                mm(pm[:, oc * 5:oc * 5 + 5], src[:, kk * 128:(kk + 1) * 128], CT[:, kc, :], kc == 0, kc == 7,
                   [wk, wk2, "ct"], [pmk])
        for j in range(5):
            tt("dve", MODR[:, li, :, j], pm[:, 0:240].rearrange("p (o j) -> p o j", j=5)[:, :, j], BMOD[:, li, :],
               ALU.add, [pmk, "bmod"], [("modr", li)])
        for which, (mi, gi) in enumerate(((1, C_G1), (4, C_G2))):
            for c in range(8):
                ts("dve", AMOD[:, li, which, c, :], MODR[:, li, mi * 8 + c, :], 1.0, lcol(li, gi + c),
                   ALU.add, ALU.mult, [("modr", li), "cols"], [("amod", li)])

    emit_mod(0)

    def wload(li, name):
        o, c = W_OFF[name]
        slot, key = WRr.get()
        p0 = o // CAST_PIECE
        p1 = (o + c - 1) // CAST_PIECE
        q = dmaq[0]
        dma(slot[:, 0:c], wbf_d[li, :, o:o + c], [("wbf", li, p) for p in range(p0, p1 + 1)], [key], eng=q)
        return slot, key

    def slot(i):
        return AR[:, i, :]

    def skey(i, ti):
        return ("ar", i, ti)

    def xkeys(c, ti):
        return ("x", c, ti)

    def hkeys(ti):
        return [("h", c, ti) for c in range(8)]

    def modcol(li, m, c, j):
        return MODR[:, li, m * 8 + c, j:j + 1]

    def norm_phase(li, b, which, tiles=(0, 1, 2, 3, 4)):
        for ti in tiles:
            t0, n = TT[ti]
            j = b if ti < 4 else 4
            st, stk = PSx.get()
            for c in range(8):
                sq, sqk = TB.get()
                act(sq[:, :n], X[:, c, t0:t0 + n], AF.Square, [xkeys(c, ti)], [sqk])
                mm(st[:, :n], CMB[:, M_ONES, :], sq[:, :n], c == 0, c == 7, [sqk, "cmb"], [stk])
            ln, lnk = TF.get()
            act(ln[:, :n], st[:, :n], AF.Ln, [stk, "kcol"], [lnk], scale=1.0 / D, bias=kcol(K_EPS))
            rs, rsk = RSD, ("rt", 0)
            act(rs[:, :n], ln[:, :n], AF.Exp, [lnk], [rsk], scale=-0.5)
            for c in range(8):
                tmp, tk = TF.get()
                stt(tmp[:, :n], X[:, c, t0:t0 + n], AMOD[:, li, which, c, j:j + 1], rs[:, :n], ALU.mult, ALU.mult,
                    [xkeys(c, ti), ("amod", li), rsk], [tk])
                act(H[:, c, t0:t0 + n], tmp[:, :n], AF.Identity, [tk, ("modr", li)], [("h", c, ti)],
                    bias=modcol(li, 3 * which, c, j))

    class Pipe:
        items = []

        @staticmethod
        def add(stages):
            it = Pipe.items
            it.append(stages)
            k = len(it) - 1
            stages[0]()
            if k >= 2:
                it[k - 2][2]()
            if k >= 1:
                it[k - 1][1]()

        @staticmethod
        def drain():
            it = Pipe.items
            k = len(it)
            if k >= 2:
                it[k - 2][2]()
            if k >= 1:
                it[k - 1][1]()
                it[k - 1][2]()
            Pipe.items = []

    def mask_write(di, out, in_, col, reads, writes):
        e = ("act", "pool", "dve", "act")[di % 4]
        if e == "act":
            act(out, in_, AF.Identity, reads, writes, scale=col)
        else:
            ts(e, out, in_, col, None, ALU.mult, None, reads, writes)

    def project_norm(parts, M, row0, stats_m, invd, gain0, dests, rope, tis, reads_extra, ti_reads):
        rows = slice(row0, row0 + M)
        for ti in tis:
            t0, n = TT[ti]
            st_ = {}

            def s1(ti=ti, t0=t0, n=n, st_=st_):
                ps, psk = PSm.get()
                for idx, (lh, rf) in enumerate(parts):
                    mm(ps[rows, :n], lh, rf(t0, n), idx == 0, idx == len(parts) - 1, reads_extra + ti_reads(ti), [psk])
                sq, sqk = TB.get()
                act(sq[rows, :n], ps[rows, :n], AF.Square, [psk], [sqk])
                st_.update(raw=ps, rawk=psk, sq=sq, sqk=sqk)

            def s2(ti=ti, t0=t0, n=n, st_=st_):
                raw, rawk, sq, sqk = st_["raw"], st_["rawk"], st_["sq"], st_["sqk"]
                st, stk = PSx.get()
                mm(st[:, :n], CMB[:, stats_m, :], sq[:, :n], True, True, [sqk, "cmb"], [stk])
                rs, rsk = TF.get()
                act(rs[rows, :n], st[rows, :n], AF.Ln, [stk, "kcol"], [rsk], scale=invd, bias=kcol(K_EPS, rows))
                act(rs[rows, :n], rs[rows, :n], AF.Exp, [rsk], [rsk], scale=-0.5)
                if rope is None or ti == 4:
                    xh = None
                    for di, (dfn, kfn, col) in enumerate(dests):
                        g = col if rope is None else gain0
                        if rope is not None and col is not None:
                            if xh is None:
                                xh, xhk = TF.get()
                                stt(xh[rows, :n], raw[rows, :n], gain0, rs[rows, :n], ALU.mult, ALU.mult,
                                    [rawk, rsk, "cols", "dcol"], [xhk])
                            mask_write(di, dfn(t0, n), xh[rows, :n], col, [xhk, "kcol"], [kfn(ti)])
                        else:
                            stt(dfn(t0, n), raw[rows, :n], g, rs[rows, :n], ALU.mult, ALU.mult,
                                [rawk, rsk, "cols", "dcol"], [kfn(ti)])
                else:
                    pmat, tab = rope
                    xh, xhk = TF.get()
                    stt(xh[rows, :n], raw[rows, :n], gain0, rs[rows, :n], ALU.mult, ALU.mult,
                        [rawk, rsk, "cols", "dcol"], [xhk])
                    rt, rtk = RT.get()
                    dma(rt[:, 0, :n], rope_d[tab, 0, :, t0:t0 + n], [], [rtk])
                    dma(rt[:, 1, :n], rope_d[tab, 1, :, t0:t0 + n], [], [rtk])
                    st_.update(rs=rs, rsk=rsk, xh=xh, xhk=xhk, rt=rt, rtk=rtk)

            def s3(ti=ti, t0=t0, n=n, st_=st_):
                if rope is None or ti == 4:
                    return
                pmat, tab = rope
                xh, xhk, rt, rtk = st_["xh"], st_["xhk"], st_["rt"], st_["rtk"]
                t1, t1k = TF.get()
                t2, t2k = st_["rs"], st_["rsk"]
                pr, prk = PSx.get()
                mm(pr[:, :n], CMF[:, pmat, :], xh[:, :n], True, True, [xhk, "cmf"], [prk])
                tt("pool", t1[rows, :n], xh[rows, :n], rt[rows, 0, :n], ALU.mult, [xhk, rtk], [t1k])
                tt("dve", t2[rows, :n], pr[rows, :n], rt[rows, 1, :n], ALU.mult, [prk, rtk], [t2k])
                if len(dests) == 1 and dests[0][2] is None:
                    dfn, kfn, _ = dests[0]
                    tt("dve", dfn(t0, n), t1[rows, :n], t2[rows, :n], ALU.add, [t1k, t2k], [kfn(ti)])
                else:
                    tt("dve", t1[rows, :n], t1[rows, :n], t2[rows, :n], ALU.add, [t1k, t2k], [t1k])
                    for di, (dfn, kfn, col) in enumerate(dests):
                        mask_write(di, dfn(t0, n), t1[rows, :n], col, [t1k, "kcol"], [kfn(ti)])

            Pipe.add((s1, s2, s3))

    def hparts(wt, M, c0=0):
        return [(wt[:, c0 + kc * M:c0 + (kc + 1) * M], (lambda kc: lambda t0, n: H[:, kc, t0:t0 + n])(kc)) for kc in range(8)]

    def sdest(si, rows=slice(0, 128)):
        return (lambda t0, n: AR[rows, si, t0:t0 + n]), (lambda ti: skey(si, ti))

    ALL_TI = [0, 1, 2, 3, 4]

    def v_project(parts_fn, ncols, pairs):
        Pipe.drain()
        VA = AR[:, 6:9, :].rearrange("p s t -> p (s t)").rearrange("p (k c) -> p k c", c=384)
        for tk in range(18):
            ti = min(tk // 4, 4)
            ps, psk = PSm.get()
            parts, rd = parts_fn(tk * 128, ti)
            for idx, (lh, rh) in enumerate(parts):
                mm(ps[:, :ncols], lh, rh, idx == 0, idx == len(parts) - 1, rd, [psk])
            src = ps[:, :ncols].rearrange("p (a b d) -> p a b d", b=2, d=64)
            dst = VA[:, tk, 0:pairs * 192].rearrange("p (a r) -> p a r", r=192)
            for b2 in range(2):
                if tk % 2 == 0:
                    op("act", (lambda b2, src, dst: lambda e: e.copy(out=dst[:, :, b2 * 128:b2 * 128 + 64], in_=src[:, :, b2, :]))(b2, src, dst),
                       [psk], [("va", tk)])
                else:
                    op("dve", (lambda b2, src, dst: lambda e: e.tensor_copy(out=dst[:, :, b2 * 128:b2 * 128 + 64], in_=src[:, :, b2, :]))(b2, src, dst),
                       [psk], [("va", tk)])
        return VA

    def attention(li, q_fn, k_fn, VA, pair, odd, qtiles, out_fn, q_reads, k_reads, head=None, hook=None):
        col0 = pair * 192 + (64 if odd else 0)
        num = slice(64, 128) if odd else slice(0, 64)
        den = slice(0, 64) if odd else slice(64, 128)
        for (ti, kts, pat0) in qtiles:
            t0, n = TT[ti]
            o, ok = PSo.get()
            sb = {}

            def emitS(idx):
                ps, psk = PSS.get()
                kt = kts[idx]
                mm(ps[:, :n], k_fn(kt), q_fn(t0, n), True, True, q_reads(ti) + k_reads(kt), [psk])
                sb[idx] = (ps, psk)

            emitS(0)
            if len(kts) > 1:
                emitS(1)
            for idx, kt in enumerate(kts):
                if idx + 2 < len(kts):
                    emitS(idx + 2)
                ps, psk = sb.pop(idx)
                p, pk = TB.get()
                if pat0 is not None and kt < 16:
                    pi = NA_PAT_IDX[(ti, kt)]
                    f0 = 10 - 2 * kt + 8 * ti
                    tmp, tmpk = TF.get()
                    tmp3 = tmp[:, :n].rearrange("p (r c) -> p r c", c=64)
                    nat3 = NAT[:, f0 * 64:f0 * 64 + 512].rearrange("p (r c) -> p r c", c=64)
                    rmb = RMt[:, pi * 8:pi * 8 + 8].unsqueeze(2).to_broadcast([128, 8, 64])
                    op("pool", (lambda tmp3, nat3, rmb: lambda e: e.tensor_tensor(out=tmp3, in0=nat3, in1=rmb, op=ALU.add))(tmp3, nat3, rmb),
                       [NATK, "rmt"], [tmpk])
                    tt("dve", tmp[:, :n], ps[:, :n], tmp[:, :n], ALU.add, [psk, tmpk], [tmpk])
                    act(p[:, :n], tmp[:, :n], AF.Exp, [tmpk], [pk])
                else:
                    act(p[:, :n], ps[:, :n], AF.Exp, [psk], [pk])
                mm(o[:, :n], VA[:, kt, col0:col0 + 128], p[:, :n], idx == 0, idx == len(kts) - 1,
                   [pk, ("va", kt)], [ok])
                if hook is not None and idx == min(12, len(kts) - 1):
                    hook()
                    hook = None
            rd, rdk = TF.get()
            if head is not None:
                act(rd[den, :n], o[den, :n], AF.Ln, [ok], [rdk])
                act(rd[den, :n], rd[den, :n], AF.Exp, [rdk], [rdk], scale=-1.0)
            else:
                op("dve", (lambda rd, o, n: lambda e: e.reciprocal(out=rd[den, :n], in_=o[den, :n]))(rd, o, n), [ok], [rdk])
            out_fn(ti, t0, n, o, ok, rd, rdk, num, den)


    def branch_out_writer(i_br, c2):
        state = {}

        def out_fn(ti, t0, n, o, ok, rd, rdk, num, den):
            state[ti] = True
            yt, ytk = YST[:, ti, :], ("yst", ti)
            tt("dve", yt[num, :n], o[num, :n], rd[den, :n], ALU.mult, [ok, rdk], [ytk])

        def flush(ti):
            t0, n = TT[ti]
            state.pop(ti)
            yt, ytk = YST[:, ti, :], ("yst", ti)
            dma(ybr_d[ti, :, i_br * 2 + c2, 0:n], yt[:, :n], [ytk], [("ybr", ti)])
        return out_fn, flush

    full_kts = list(range(18))
    ctx_kts = [16, 17]

    def std_qtiles():
        return [(ti, full_kts, None) for ti in range(4)] + [(4, ctx_kts, None)]

    def layer(li, b):
        mode_dense()
        if b == 0 and li + 1 < L:
            emit_cast(li + 1)
            emit_mod(li + 1)
        norm_phase(li, b, 0)
        mode_proj()
        hr = lambda ti: hkeys(ti)
        VAv = AR[:, 6:9, :].rearrange("p s t -> p (s t)").rearrange("p (k c) -> p k c", c=384)
        for pr_ in range(2):
            op("pool", (lambda pr_: lambda e: e.memset(VAv[:, :, pr_ * 192 + 64:pr_ * 192 + 128], 1.0))(pr_), [],
               [("va", i) for i in range(18)] + [skey(s_, tq) for s_ in (6, 7, 8) for tq in range(5)])
        for c in range(2):
            wq, wqk = wload(li, f"na_q{c}")
            project_norm(hparts(wq, 128), 128, 0, M_O64, 1.0 / 64, None,
                         [sdest(c * 3 + 0) + (dcol(li, DC_NA_LO),), sdest(c * 3 + 1) + (dcol(li, DC_NA_HI),)],
                         None, ALL_TI, [wqk], hr)
            wk_, wkk = wload(li, f"na_k{c}")
            project_norm(hparts(wk_, 128), 128, 0, M_O64, 1.0 / 64, None,
                         [sdest(c * 3 + 2) + (lcol(li, C_NA_GK),)], None, ALL_TI, [wkk], hr)
        wva, wvak = wload(li, "na_va")
        wvb, wvbk = wload(li, "na_vb")

        def na_vparts(tk0, ti):
            ps = []
            for kc in range(8):
                w = wva if kc < 4 else wvb
                ps.append((H[:, kc, tk0:tk0 + 128], w[:, (kc % 4) * 256:(kc % 4 + 1) * 256]))
            return ps, [wvak, wvbk] + hkeys(ti)

        VA = v_project(na_vparts, 256, 2)
        mode_attn()
        for c in range(2):
            ofn, flush = branch_out_writer(0, c)
            qt = [(ti, NA_KT[ti] + ctx_kts, 1) for ti in range(4)] + [(4, ctx_kts, None)]
            for hh in range(2):
                si = c * 3 + hh
                dma(NAT, nab_d[li, 2 * c + hh, :, :], [], [NATK])
                attention(li, (lambda si: lambda t0, n: AR[:, si, t0:t0 + n])(si),
                          (lambda c: lambda kt: AR[:, c * 3 + 2, kt * 128:(kt + 1) * 128])(c),
                          VA, c, hh == 1, qt, ofn,
                          (lambda si: lambda ti_: [skey(si, ti_)])(si),
                          (lambda c: lambda kt: [skey(c * 3 + 2, min(kt // 4, 4))])(c), head=2 * c + hh)
            for ti in range(5):
                flush(ti)
        mode_proj()
        wva, wvak = wload(li, "df_va")
        wvb, wvbk = wload(li, "df_vb")

        def df_vparts(tk0, ti):
            ps = []
            for kc in range(8):
                w = wva if kc < 4 else wvb
                ps.append((H[:, kc, tk0:tk0 + 128], w[:, (kc % 4) * 256:(kc % 4 + 1) * 256]))
            return ps, [wvak, wvbk] + hkeys(ti)

        VA = v_project(df_vparts, 256, 2)
        for c in range(2):
            mode_proj()
            wq, wqk = wload(li, f"df_q{c}")
            project_norm(hparts(wq, 128), 128, 0, M_O32, 1.0 / 32, dcol(li, DC_DF_GQ),
                         [sdest(g) + (kcol(K_DM0 + g),) for g in range(4)], (M_PDF, 0), ALL_TI, [wqk], hr)
            wk_, wkk = wload(li, f"df_k{c}")
            project_norm(hparts(wk_, 128), 128, 0, M_O32, 1.0 / 32, lcol(li, C_DF_GK),
                         [sdest(4) + (None,)], (M_PDF, 0), ALL_TI, [wkk], hr)
            mode_attn()
            state = {}

            def df_out(ti, t0, n, o, ok, rd, rdk, num, den, state=state):
                ym, ymk = TF.get()
                tt("dve", ym[num, :n], o[num, :n], rd[den, :n], ALU.mult, [ok, rdk], [ymk])
                state.setdefault(ti, []).append((ym, ymk))

            pending = []

            def df_post(ti, hh, y0, y0k, y1, y1k, c=c):
                t0, n = TT[ti]
                num = slice(64, 128) if hh else slice(0, 64)
                yt, ytk = YST[:, ti, :], ("yst", ti)
                yd, ydk = TF.get()
                stt(yd[num, :n], y1[num, :n], dcol(li, DC_NEGLAM, num), y0[num, :n], ALU.mult, ALU.add,
                    [y0k, y1k, "dcol"], [ydk])
                sq, sqk = TB.get()
                tt("pool", sq[:, :n], yd[:, :n], yd[:, :n], ALU.mult, [ydk], [sqk])
                st, stk = PSx.get()
                mm(st[:, :n], CMB[:, M_O64, :], sq[:, :n], True, True, [sqk, "cmb"], [stk])
                ln, lnk = TF.get()
                act(ln[num, :n], st[num, :n], AF.Ln, [stk, "kcol"], [lnk], scale=1.0 / 64, bias=kcol(K_EPS, num))
                act(ln[num, :n], ln[num, :n], AF.Exp, [lnk], [lnk], scale=-0.5)
                stt(yt[num, :n], yd[num, :n], dcol(li, DC_DF_SUB, num), ln[num, :n], ALU.mult, ALU.mult,
                    [ydk, lnk, "dcol"], [ytk])
                if hh == 1:
                    dma(ybr_d[ti, :, 1 * 2 + c, 0:n], yt[:, :n], [ytk], [("ybr", ti)])

            for ti in range(5):
                kts = full_kts if ti < 4 else ctx_kts
                for hh in range(2):
                    for m in range(2):
                        g = hh * 2 + m
                        hk = None
                        if m == 0 and pending:
                            hk = pending.pop(0)
                        attention(li, (lambda g: lambda t0_, n_: AR[:, g, t0_:t0_ + n_])(g),
                                  lambda kt: AR[:, 4, kt * 128:(kt + 1) * 128],
                                  VA, c, hh == 1, [(ti, kts, None)], df_out,
                                  (lambda g: lambda ti_: [skey(g, ti_)])(g),
                                  lambda kt: [skey(4, min(kt // 4, 4))], hook=hk)
                    (y0, y0k), (y1, y1k) = state.pop(ti)
                    pending.append((lambda ti, hh, y0, y0k, y1, y1k: lambda: df_post(ti, hh, y0, y0k, y1, y1k))(ti, hh, y0, y0k, y1, y1k))
            while pending:
                pending.pop(0)()
        mode_proj()
        wv, wvk = wload(li, "gq_v")

        def gq_vparts(tk0, ti):
            return [(H[:, kc, tk0:tk0 + 128], wv[:, kc * 128:(kc + 1) * 128]) for kc in range(8)], [wvk] + hkeys(ti)

        VA = v_project(gq_vparts, 128, 1)
        wk_, wkk = wload(li, "gq_k")
        project_norm(hparts(wk_, 128), 128, 0, M_O64, 1.0 / 64, lcol(li, C_GQ_GK),
                     [sdest(4) + (None,)], (M_PGQ, 1), ALL_TI, [wkk], hr)
        for ci, cn in enumerate(("qA", "qB")):
            wq, wqk = wload(li, f"gq_{cn}")
            project_norm(hparts(wq, 128), 128, 0, M_O64, 1.0 / 64, dcol(li, DC_GQ_GQ),
                         [sdest(ci * 2 + 0) + (kcol(K_MLO),), sdest(ci * 2 + 1) + (kcol(K_MHI),)],
                         (M_PGQ, 1), ALL_TI, [wqk], hr)
        mode_attn()
        for ci in range(2):
            ofn, flush = branch_out_writer(2, ci)
            for (ti, kts, pat0) in std_qtiles():
                for hh in range(2):
                    si = ci * 2 + hh
                    attention(li, (lambda si: lambda t0, n: AR[:, si, t0:t0 + n])(si),
                              lambda kt: AR[:, 4, kt * 128:(kt + 1) * 128],
                              VA, 0, hh == 1, [(ti, kts, None)], ofn,
                              (lambda si: lambda ti_: [skey(si, ti_)])(si),
                              lambda kt: [skey(4, min(kt // 4, 4))])
                flush(ti)
        mode_proj()
        w0, w0k = wload(li, "ml_qa0")
        w1, w1k = wload(li, "ml_qa1")
        for ti in ALL_TI:
            t0, n = TT[ti]
            pa, pak = PSm.get()
            for kc in range(8):
                mm(pa[:, :n], w0[:, kc * 128:(kc + 1) * 128], H[:, kc, t0:t0 + n], kc == 0, kc == 7, [w0k] + hkeys(ti), [pak])
            ra, rak = TF.get()
            act(ra[:, :n], pa[:, :n], AF.Copy, [pak], [rak])
            pb, pbk = PSm.get()
            for kc in range(8):
                mm(pb[0:64, :n], w1[:, kc * 64:(kc + 1) * 64], H[:, kc, t0:t0 + n], kc == 0, kc == 7, [w1k] + hkeys(ti), [pbk])
            rb, rbk = TF.get()
            act(rb[0:64, :n], pb[0:64, :n], AF.Copy, [pbk], [rbk])
            sqa, sqak = TB.get()
            tt("pool", sqa[:, :n], ra[:, :n], ra[:, :n], ALU.mult, [rak], [sqak])
            sqb, sqbk = TB.get()
            tt("pool", sqb[0:64, :n], rb[0:64, :n], rb[0:64, :n], ALU.mult, [rbk], [sqbk])
            st, stk = PSx.get()
            mm(st[:, :n], CMB[:, M_ONES, :], sqa[:, :n], True, False, [sqak, "cmb"], [stk])
            mm(st[:, :n], CMB[0:64, M_ONES, :], sqb[0:64, :n], False, True, [sqbk, "cmb"], [stk])
            ln, lnk = TF.get()
            act(ln[:, :n], st[:, :n], AF.Ln, [stk, "kcol"], [lnk], scale=1.0 / 192, bias=kcol(K_EPS))
            act(ln[:, :n], ln[:, :n], AF.Exp, [lnk], [lnk], scale=-0.5)
            stt(AR[:, 0, t0:t0 + n], ra[:, :n], lcol(li, C_ML_GQA0), ln[:, :n], ALU.mult, ALU.mult, [rak, lnk, "cols"], [skey(0, ti)])
            stt(AR[0:64, 1, t0:t0 + n], rb[0:64, :n], lcol(li, C_ML_GQA1, slice(0, 64)), ln[0:64, :n], ALU.mult, ALU.mult,
                [rbk, lnk, "cols"], [skey(1, ti)])
        wkv, wkvk = wload(li, "ml_kva")
        project_norm(hparts(wkv, 128), 128, 0, M_ONES, 1.0 / 128, None, [sdest(2) + (lcol(li, C_ML_GKVA),)], None,
                     ALL_TI, [wkvk], hr)
        wkr, wkrk = wload(li, "ml_kr")
        project_norm(hparts(wkr, 32), 32, 64, M_OML, 1.0 / 32, lcol(li, C_ML_GK, slice(64, 96)),
                     [sdest(3, slice(64, 96)) + (None,)], (M_PML, 2), ALL_TI, [wkrk], hr)
        w2, w2k = wload(li, "ml_2nd")
        Pipe.drain()

        def ml_vparts(tk0, ti):
            return [(AR[:, 2, tk0:tk0 + 128], w2[:, 1024:1280])], [w2k, skey(2, ti)]

        VA = v_project(ml_vparts, 256, 2)
        for c2 in range(2):
            ofn, flush = branch_out_writer(3, c2)
            qs, ks = 4, 5
            for hh in range(2):
                h = c2 * 2 + hh
                r96 = slice(0, 96)
                mode_proj()
                qparts = [(w2[:, h * 192:h * 192 + 96], lambda t0, n: AR[:, 0, t0:t0 + n]),
                          (w2[0:64, h * 192 + 96:h * 192 + 192], lambda t0, n: AR[0:64, 1, t0:t0 + n])]
                project_norm(qparts, 96, 0, M_OML, kcol(K_INVD_ML, r96), dcol(li, DC_ML_GQ, r96),
                             [sdest(qs, r96) + (None,)], (M_PML, 2), ALL_TI, [w2k],
                             lambda ti: [skey(0, ti), skey(1, ti)])
                kparts = [(w2[:, 768 + h * 64:768 + (h + 1) * 64], lambda t0, n: AR[:, 2, t0:t0 + n])]
                r64 = slice(0, 64)
                project_norm(kparts, 64, 0, M_OML, 1.0 / 64, None,
                             [sdest(ks, r64) + (lcol(li, C_ML_GK, r64),)], None, ALL_TI, [w2k],
                             lambda ti: [skey(2, ti)])
                mode_attn()
                for ti in ALL_TI:
                    t0, n = TT[ti]
                    op("pool", (lambda t0, n: lambda e: e.tensor_copy(out=AR[64:96, ks, t0:t0 + n], in_=AR[64:96, 3, t0:t0 + n]))(t0, n),
                       [skey(3, ti)], [skey(ks, ti)])
                attention(li, lambda t0, n: AR[0:96, qs, t0:t0 + n],
                          lambda kt: AR[0:96, ks, kt * 128:(kt + 1) * 128],
                          VA, c2, hh == 1, std_qtiles(), ofn,
                          lambda ti_: [skey(qs, ti_)],
                          lambda kt: [skey(ks, min(kt // 4, 4))])
            for ti in ALL_TI:
                flush(ti)
        mode_dense()
        for ti, (t0, n) in enumerate(TT):
            j = b if ti < 4 else 4
            yin = AR[:, 2 + 2 * (ti % 2):4 + 2 * (ti % 2), :].rearrange("p s t -> p (s t)")[:, 0:4096].rearrange("p (c t) -> p c t", t=512)
            yink = [skey(2 + 2 * (ti % 2), tq) for tq in range(5)] + [skey(3 + 2 * (ti % 2), tq) for tq in range(5)]
            dma(yin[:, :, :], ybr_d[ti, :, :, :], [("ybr", ti)], yink)
            ym = AR[:, 0:2, :].rearrange("p s t -> p (s t)")[:, 0:4096].rearrange("p (c t) -> p c t", t=512)
            ymk = [skey(0, tq) for tq in range(5)] + [skey(1, tq) for tq in range(5)]
            for oc in range(8):
                acc = None
                for i in range(4):
                    wt, wtk = wload(li, f"gb_{i}_{oc}")
                    pg, pgk = PSm.get()
                    for kc in range(8):
                        mm(pg[:, :n], wt[:, kc * 128:(kc + 1) * 128], H[:, kc, t0:t0 + n], kc == 0, kc == 7,
                           [wtk] + hkeys(ti), [pgk])
                    sg, sgk = TF.get()
                    act(sg[:, :n], pg[:, :n], AF.Sigmoid, [pgk], [sgk])
                    pb, pbk = PSm.get()
                    for c2 in range(2):
                        mm(pb[:, :n], wt[:, 1024 + c2 * 128:1024 + (c2 + 1) * 128], yin[:, i * 2 + c2, :n], c2 == 0, c2 == 1,
                           [wtk] + yink, [pbk])
                    if i == 0:
                        acc, acck = TF.get()
                        tt("dve", acc[:, :n], pb[:, :n], sg[:, :n], ALU.mult, [pbk, sgk], [acck])
                    else:
                        tt("dve", sg[:, :n], pb[:, :n], sg[:, :n], ALU.mult, [pbk, sgk], [sgk])
                        if i < 3:
                            tt("pool", acc[:, :n], acc[:, :n], sg[:, :n], ALU.add, [acck, sgk], [acck])
                        else:
                            tt("pool", ym[:, oc, :n], acc[:, :n], sg[:, :n], ALU.add, [acck, sgk], ymk)
            for oc2 in range(8):
                wt, wtk = wload(li, f"wo_{oc2}")
                po, pok = PSm.get()
                for oc in range(8):
                    mm(po[:, :n], wt[:, oc * 128:(oc + 1) * 128], ym[:, oc, :n], oc == 0, oc == 7, [wtk] + ymk, [pok])
                stt(X[:, oc2, t0:t0 + n], po[:, :n], modcol(li, 2, oc2, j), X[:, oc2, t0:t0 + n], ALU.mult, ALU.add,
                    [pok, ("modr", li), xkeys(oc2, ti)], [xkeys(oc2, ti)])
            if not skip_ffn and _os.environ.get("NORM2_INLOOP", "1") == "1":
                norm_phase(li, b, 1, tiles=(ti,))
        if skip_ffn:
            return
        if _os.environ.get("NORM2_INLOOP", "1") != "1":
            norm_phase(li, b, 1)
        U = AR[:, 0:4, :].rearrange("p s t -> p (s t)").bitcast(F32).rearrange("p (k t) -> p k t", k=2)
        ukeys = [[skey(2 * kd, tq) for tq in range(5)] + [skey(2 * kd + 1, tq) for tq in range(5)] for kd in range(2)]
        for g, js in enumerate(FF_GROUPS):
            for jj, jx in enumerate(js):
                ws = []
                for kd, nm in enumerate(("up_g", "up_v")):
                    ws.append(wload(li, f"{nm}_{jx}"))
                ccs = {}

                def taps(ti, jj=jj, jx=jx, ccs=ccs):
                    t0, n = TT[ti]
                    s0, s1 = (0, SEQ) if ti < 4 else (SEQ, T)
                    for kd in range(2):
                        base = C_CONV + (kd * NJ + jx) * 4
                        cc, cck = ccs[(kd, ti)]
                        lo = max(t0, s0 + 1)
                        stt(cc[:, lo - t0:n], U[:, kd, lo - 1:t0 + n - 1], lcol(li, base + 0), cc[:, lo - t0:n], ALU.mult, ALU.add,
                            ukeys[kd] + ["cols", cck], [cck])
                        hi = min(t0 + n, s1 - 1)
                        stt(cc[:, 0:hi - t0], U[:, kd, t0 + 1:hi + 1], lcol(li, base + 2), cc[:, 0:hi - t0], ALU.mult, ALU.add,
                            ukeys[kd] + ["cols", cck], [cck])
                    sg, sgk = TF.get()
                    act(sg[:, :n], ccs[(0, ti)][0][:, :n], AF.Silu, [ccs[(0, ti)][1]], [sgk])
                    tt("pool", AR[:, 4 + jj, t0:t0 + n], sg[:, :n], ccs[(1, ti)][0][:, :n], ALU.mult,
                       [sgk, ccs[(1, ti)][1]], [skey(4 + jj, ti)])

                for ti, (t0, n) in enumerate(TT):
                    for kd in range(2):
                        wt, wtk = ws[kd]
                        pu, puk = PSm.get()
                        for kc in range(8):
                            mm(pu[:, :n], wt[:, kc * 128:(kc + 1) * 128], H[:, kc, t0:t0 + n], kc == 0, kc == 7,
                               [wtk] + hkeys(ti), [puk])
                        base = C_CONV + (kd * NJ + jx) * 4
                        cc, cck = TF.get()
                        ccs[(kd, ti)] = (cc, cck)
                        act(U[:, kd, t0:t0 + n], pu[:, :n], AF.Copy, [puk], ukeys[kd])
                        if _os.environ.get("FFN_ACT_CENTER", "1") == "1":
                            act(cc[:, :n], pu[:, :n], AF.Identity, [puk, "cols"], [cck], scale=lcol(li, base + 1),
                                bias=lcol(li, base + 3))
                        else:
                            ts("dve", cc[:, :n], pu[:, :n], lcol(li, base + 1), lcol(li, base + 3), ALU.mult, ALU.add,
                               [puk, "cols"], [cck])
                    if ti >= 1:
                        taps(ti - 1)
                taps(4)
            for oc in range(8):
                wt, wtk = wload(li, f"dn_{g}_{oc}")
                for ti, (t0, n) in enumerate(TT):
                    j = b if ti < 4 else 4
                    pd, pdk = PSm.get()
                    for jj in range(len(js)):
                        mm(pd[:, :n], wt[:, jj * 128:(jj + 1) * 128], AR[:, 4 + jj, t0:t0 + n], jj == 0, jj == len(js) - 1,
                           [wtk, skey(4 + jj, ti)], [pdk])
                    stt(X[:, oc, t0:t0 + n], pd[:, :n], modcol(li, 5, oc, j), X[:, oc, t0:t0 + n], ALU.mult, ALU.add,
                        [pdk, ("modr", li), xkeys(oc, ti)], [xkeys(oc, ti)])

    fin = []
    for b in range(NB):
        for c in range(8):
            dma(X[:, c, :], xT_d[b, :, c, :], [], [xkeys(c, ti) for ti in range(5)])
        for li in range(L):
            layer(li, b)
        for c in range(8):
            fin.append(dma(yT_d[b, :, c, :], X[:, c, :], [xkeys(c, ti) for ti in range(5)], []))
    S.emit(final_wait_ops=fin)
    S.close()
    return nc


def prepare_shared(layers, w_mod, b_mod, norm1_g, norm2_g, w_in, na_qk_g, na_rel_bias, diff_qk_g, diff_lambda,
                   diff_subln_g, gqa_qk_g, mla_qa_g, mla_kva_g, w_mla_qb, w_mla_kvb, mla_qk_g, w_gate, w_branch,
                   w_out, w_up, conv_w, conv_b, w_down):
    f = lambda a: np.asarray(a, dtype=np.float32)
    (w_mod, b_mod, norm1_g, norm2_g, w_in, na_qk_g, na_rel_bias, diff_qk_g, diff_lambda, diff_subln_g, gqa_qk_g,
     mla_qa_g, mla_kva_g, w_mla_qb, w_mla_kvb, mla_qk_g, w_gate, w_branch, w_out, w_up, conv_w, conv_b, w_down) = map(
        f, (w_mod, b_mod, norm1_g, norm2_g, w_in, na_qk_g, na_rel_bias, diff_qk_g, diff_lambda, diff_subln_g, gqa_qk_g,
            mla_qa_g, mla_kva_g, w_mla_qb, w_mla_kvb, mla_qk_g, w_gate, w_branch, w_out, w_up, conv_w, conv_b, w_down))
    L = len(layers)
    sh = {}
    wm = np.stack([w_mod[l] for l in layers])
    wm = wm.reshape(L, 8, 128, 48, 128).transpose(0, 3, 2, 1, 4).reshape(L, 48, 128, 1024)
    sh["wmod"] = np.ascontiguousarray(wm)
    bm = np.stack([b_mod[l] for l in layers])
    sh["bmod"] = np.ascontiguousarray(bm.reshape(L, 48, 128).transpose(2, 0, 1))
    sh["wblob"] = np.stack([_build_blob(l, w_in, w_mla_qb, w_mla_kvb, w_gate, w_branch, w_out, w_up, w_down)
                            for l in layers])
    cols = np.stack([_build_cols(l, norm1_g, norm2_g, na_qk_g, diff_qk_g, diff_subln_g, gqa_qk_g, mla_qa_g,
                                 mla_kva_g, mla_qk_g, conv_w, conv_b) for l in layers])
    sh["cols"] = np.ascontiguousarray(cols.transpose(1, 0, 2))
    sh["kcol"] = _const_cols()
    sh["cmat"] = _const_mats()
    sh["rope"] = _rope_tables()
    sh["nab"] = np.ascontiguousarray(_na_bias_tables(na_rel_bias)[layers]).reshape(L, 4, 128, NA_NF * 64)
    sh["narm"] = np.ascontiguousarray(_na_row_masks().reshape(128, len(NA_PATS) * 8))
    lam = np.stack([diff_lambda[l].reshape(128) for l in layers])
    sh["lam"] = np.ascontiguousarray(np.broadcast_to(lam[None], (128, L, 128)))
    return sh


def lambda_init_of(i):
    return 0.8 - 0.6 * math.exp(-0.3 * i)


def run_layers(x, ctx, c, c_ctx, layers, weights, ncores=NCORES, debug=False, skip_ffn=False):
    B = x.shape[0]
    NB = B // ncores
    sh = prepare_shared(layers, **weights)
    lam = [lambda_init_of(l) for l in layers]
    nc = build_program(NB, layers=list(layers), lambda_inits=lam, debug=debug, skip_ffn=skip_ffn)
    in_maps = []
    for k in range(ncores):
        xs = np.concatenate([x[k * NB:(k + 1) * NB], ctx[k * NB:(k + 1) * NB]], axis=1)
        xT = np.ascontiguousarray(xs.reshape(NB, T, 8, 128).transpose(0, 3, 2, 1))
        cv = np.concatenate([c[k * NB:(k + 1) * NB], c_ctx[None, :]], axis=0)
        if NB < 4:
            cv = np.concatenate([cv[:NB], np.zeros((4 - NB, D), np.float32), cv[NB:]], axis=0)
        cT = np.ascontiguousarray(cv.reshape(5, 8, 128).transpose(2, 1, 0))
        m = dict(sh)
        m["xT"] = xT
        m["cT"] = cT
        in_maps.append(m)
    res = run_bass_kernel_spmd(nc, in_maps, core_ids=list(range(ncores)))
    outs = []
    for k in range(ncores):
        yT = res.results[k]["yT"]
        outs.append(yT.transpose(0, 3, 2, 1).reshape(NB, T, D))
    o = np.concatenate(outs, axis=0)
    if debug:
        return o[:, :SEQ], o[:, SEQ:], res.results[0]["ybr"]
    return o[:, :SEQ], o[:, SEQ:]


def kernel(x, c, ctx, c_ctx, w_mod, b_mod, norm1_g, norm2_g, w_in, na_qk_g, na_rel_bias,
           diff_qk_g, diff_lambda, diff_subln_g, gqa_qk_g, mla_qa_g, mla_kva_g, w_mla_qb, w_mla_kvb,
           mla_qk_g, w_gate, w_branch, w_out, w_up, conv_w, conv_b, w_down):
    weights = dict(w_mod=w_mod, b_mod=b_mod, norm1_g=norm1_g, norm2_g=norm2_g, w_in=w_in, na_qk_g=na_qk_g,
                   na_rel_bias=na_rel_bias, diff_qk_g=diff_qk_g, diff_lambda=diff_lambda, diff_subln_g=diff_subln_g,
                   gqa_qk_g=gqa_qk_g, mla_qa_g=mla_qa_g, mla_kva_g=mla_kva_g, w_mla_qb=w_mla_qb, w_mla_kvb=w_mla_kvb,
                   mla_qk_g=mla_qk_g, w_gate=w_gate, w_branch=w_branch, w_out=w_out, w_up=w_up, conv_w=conv_w,
                   conv_b=conv_b, w_down=w_down)
    x = np.asarray(x, np.float32)
    ctx = np.asarray(ctx, np.float32)
    c = np.asarray(c, np.float32)
    c_ctx = np.asarray(c_ctx, np.float32)
    out, _ = run_layers(x, ctx, c, c_ctx, list(range(DEPTH)), weights)
    return np.ascontiguousarray(out, dtype=np.float32)
```
